# Optimizing a Trainium2 kernel written in Bass

```python
import math
import jax, jax.numpy as jnp
from jax import lax
import numpy as np

D_MODEL = 1024
BATCH = 2
SEQ = 8192
DEPTH = 2

D_FF = 2816
EPS = 1e-6
Q_BLOCK = 128
NEG_INF = -1e30

MLA_HEADS = 8
MLA_NOPE_DIM = 64
MLA_ROPE_DIM = 32
MLA_V_DIM = 64
MLA_Q_RANK = 256
MLA_KV_RANK = 128
ROPE_BASE = 10000.0

DIFF_HEADS = 8
DIFF_HEAD_DIM = 32

POOL_WINDOWS = (2, 4, 8, 16)
POOL_GROUP = 128
POOL_WIDTH = POOL_GROUP * len(POOL_WINDOWS)

N_BRANCH = 3
BRANCH_WIDTH = 512

DIFF_QK = DIFF_HEADS * 2 * DIFF_HEAD_DIM
DIFF_V = DIFF_HEADS * 2 * DIFF_HEAD_DIM
GATE_WIDTH = N_BRANCH * D_MODEL
IN_SPLITS = (MLA_Q_RANK, MLA_KV_RANK, MLA_ROPE_DIM, DIFF_QK, DIFF_QK, DIFF_V, POOL_WIDTH, GATE_WIDTH)
IN_WIDTH = sum(IN_SPLITS)

kernel_name = "hybrid_mla_diffattn_pool_macaron"


def rmsnorm(x, g):
    xf = x.astype(jnp.float32)
    y = xf * lax.rsqrt(jnp.mean(xf * xf, axis=-1, keepdims=True) + EPS)
    return (y * g.astype(jnp.float32)).astype(x.dtype)


def swiglu(x, w_gate, w_up, w_down):
    return (jax.nn.silu(x @ w_gate) * (x @ w_up)) @ w_down


def rope(x, pos):
    half = x.shape[-1] // 2
    inv_freq = ROPE_BASE ** (-jnp.arange(half, dtype=jnp.float32) / half)
    ang = pos.astype(jnp.float32)[:, :, None, None] * inv_freq
    cos, sin = jnp.cos(ang), jnp.sin(ang)
    xf = x.astype(jnp.float32)
    x1, x2 = xf[..., :half], xf[..., half:]
    return jnp.concatenate([x1 * cos - x2 * sin, x2 * cos + x1 * sin], axis=-1).astype(x.dtype)


def sweep_query_blocks(fn, *qs):
    b, s = qs[0].shape[:2]
    nb = s // Q_BLOCK
    blocked = tuple(jnp.moveaxis(q.reshape((b, nb, Q_BLOCK) + q.shape[2:]), 1, 0) for q in qs)
    starts = jnp.arange(nb, dtype=jnp.int32) * Q_BLOCK
    out = lax.map(lambda a: fn(a[0], *a[1]), (starts, blocked))
    out = jnp.moveaxis(out, 0, 1)
    return out.reshape((b, s) + out.shape[3:])


def causal_mask(start, s):
    q_idx = start + jnp.arange(Q_BLOCK, dtype=jnp.int32)
    k_idx = jnp.arange(s, dtype=jnp.int32)
    return k_idx[None, :] <= q_idx[:, None]


def mla_mixer(c_q, c_kv, k_rope_raw, pos, q_norm, w_uq, kv_norm, w_ukv):
    b, s, _ = c_q.shape
    q = (rmsnorm(c_q, q_norm) @ w_uq).reshape(b, s, MLA_HEADS, MLA_NOPE_DIM + MLA_ROPE_DIM)
    q_nope, q_rope = q[..., :MLA_NOPE_DIM], rope(q[..., MLA_NOPE_DIM:], pos)
    kv = (rmsnorm(c_kv, kv_norm) @ w_ukv).reshape(b, s, MLA_HEADS, MLA_NOPE_DIM + MLA_V_DIM)
    k_nope, v = kv[..., :MLA_NOPE_DIM], kv[..., MLA_NOPE_DIM:]
    k_rope = rope(k_rope_raw[:, :, None, :], pos)[:, :, 0]
    scale = (MLA_NOPE_DIM + MLA_ROPE_DIM) ** -0.5

    def block(start, qn, qr):
        sc = jnp.einsum('bqhd,bkhd->bhqk', qn, k_nope) + jnp.einsum('bqhd,bkd->bhqk', qr, k_rope)
        sc = jnp.where(causal_mask(start, s), sc.astype(jnp.float32) * scale, NEG_INF)
        p = jax.nn.softmax(sc, axis=-1).astype(v.dtype)
        return jnp.einsum('bhqk,bkhd->bqhd', p, v)

    o = sweep_query_blocks(block, q_nope, q_rope)
    return o.reshape(b, s, MLA_HEADS * MLA_V_DIM)


def diff_mixer(q, k, v, pos, lq1, lk1, lq2, lk2, subln, lambda_init):
    b, s, _ = q.shape
    q = q.reshape(b, s, DIFF_HEADS, 2, DIFF_HEAD_DIM)
    k = k.reshape(b, s, DIFF_HEADS, 2, DIFF_HEAD_DIM)
    v = v.reshape(b, s, DIFF_HEADS, 2 * DIFF_HEAD_DIM)
    f32 = jnp.float32
    lam = (jnp.exp(jnp.sum(lq1.astype(f32) * lk1.astype(f32)))
           - jnp.exp(jnp.sum(lq2.astype(f32) * lk2.astype(f32))) + lambda_init)
    slopes = jnp.exp2(-8.0 * jnp.arange(1, DIFF_HEADS + 1, dtype=f32) / DIFF_HEADS)
    scale = DIFF_HEAD_DIM ** -0.5

    def block(start, qb, pb):
        sc = jnp.einsum('bqhmd,bkhmd->bmhqk', qb, k).astype(f32) * scale
        dist = jnp.abs(pb[:, :, None] - pos[:, None, :]).astype(f32)
        sc = sc - slopes[None, None, :, None, None] * dist[:, None, None]
        sc = jnp.where(causal_mask(start, s), sc, NEG_INF)
        p = jax.nn.softmax(sc, axis=-1)
        a = (p[:, 0] - lam * p[:, 1]).astype(v.dtype)
        return jnp.einsum('bhqk,bkhd->bqhd', a, v)

    o = sweep_query_blocks(block, q, pos)
    o = rmsnorm(o, subln) * (1.0 - lambda_init)
    return o.reshape(b, s, DIFF_HEADS * 2 * DIFF_HEAD_DIM)


def pool_mixer(p, pool_w, pool_b, pool_scale):
    b, s, _ = p.shape
    pf = p.astype(jnp.float32)
    cs = jnp.cumsum(pf, axis=1)
    t = jnp.arange(s, dtype=jnp.int32)
    outs = []
    for g, w in enumerate(POOL_WINDOWS):
        sl = slice(g * POOL_GROUP, (g + 1) * POOL_GROUP)
        c = cs[..., sl]
        prev = jnp.pad(c, ((0, 0), (w, 0), (0, 0)))[:, :s]
        cnt = jnp.minimum(t + 1, w).astype(jnp.float32)[None, :, None]
        outs.append((c - prev) / cnt - pf[..., sl])
    pooled = jnp.stack(outs, axis=2).astype(p.dtype)
    y = jnp.einsum('bsgc,gcd->bsgd', pooled, pool_w) + pool_b
    return y.reshape(b, s, POOL_WIDTH) * pool_scale


def setup_inputs(seed: int = 0) -> dict:
    key = jax.random.key(seed)
    ks = iter(jax.random.split(key, 40))
    f32 = jnp.float32

    def dense(shape, fan_in):
        return jax.random.normal(next(ks), shape, f32) * fan_in ** -0.5

    def gain(shape):
        return 1.0 + 0.02 * jax.random.normal(next(ks), shape, f32)

    x = jax.random.normal(next(ks), (BATCH, SEQ, D_MODEL), f32)
    offset = jax.random.randint(next(ks), (BATCH, 1), 0, 1024, dtype=jnp.int32)
    positions = (jnp.arange(SEQ, dtype=jnp.int32)[None, :] + offset).astype(jnp.int32)
    return {
        "x": x,
        "positions": positions,
        "ffn1_norm": gain((DEPTH, D_MODEL)),
        "ffn1_w_gate": dense((DEPTH, D_MODEL, D_FF), D_MODEL),
        "ffn1_w_up": dense((DEPTH, D_MODEL, D_FF), D_MODEL),
        "ffn1_w_down": dense((DEPTH, D_FF, D_MODEL), D_FF),
        "mix_norm": gain((DEPTH, D_MODEL)),
        "w_in": dense((DEPTH, D_MODEL, IN_WIDTH), D_MODEL),
        "mla_q_norm": gain((DEPTH, MLA_Q_RANK)),
        "mla_w_uq": dense((DEPTH, MLA_Q_RANK, MLA_HEADS * (MLA_NOPE_DIM + MLA_ROPE_DIM)), MLA_Q_RANK),
        "mla_kv_norm": gain((DEPTH, MLA_KV_RANK)),
        "mla_w_ukv": dense((DEPTH, MLA_KV_RANK, MLA_HEADS * (MLA_NOPE_DIM + MLA_V_DIM)), MLA_KV_RANK),
        "diff_lambda_q1": 0.1 * jax.random.normal(next(ks), (DEPTH, DIFF_HEAD_DIM), f32),
        "diff_lambda_k1": 0.1 * jax.random.normal(next(ks), (DEPTH, DIFF_HEAD_DIM), f32),
        "diff_lambda_q2": 0.1 * jax.random.normal(next(ks), (DEPTH, DIFF_HEAD_DIM), f32),
        "diff_lambda_k2": 0.1 * jax.random.normal(next(ks), (DEPTH, DIFF_HEAD_DIM), f32),
        "diff_subln": gain((DEPTH, 2 * DIFF_HEAD_DIM)),
        "pool_w": dense((DEPTH, len(POOL_WINDOWS), POOL_GROUP, POOL_GROUP), POOL_GROUP),
        "pool_b": 0.01 * jax.random.normal(next(ks), (DEPTH, len(POOL_WINDOWS), POOL_GROUP), f32),
        "pool_scale": 1.0 + 0.05 * jax.random.normal(next(ks), (DEPTH, POOL_WIDTH), f32),
        "w_branch": dense((DEPTH, N_BRANCH, BRANCH_WIDTH, D_MODEL), BRANCH_WIDTH),
        "w_out": dense((DEPTH, D_MODEL, D_MODEL), D_MODEL),
        "ffn2_norm": gain((DEPTH, D_MODEL)),
        "ffn2_w_gate": dense((DEPTH, D_MODEL, D_FF), D_MODEL),
        "ffn2_w_up": dense((DEPTH, D_MODEL, D_FF), D_MODEL),
        "ffn2_w_down": dense((DEPTH, D_FF, D_MODEL), D_FF),
        "final_norm": gain((D_MODEL,)),
    }


def reference(x, positions, ffn1_norm, ffn1_w_gate, ffn1_w_up, ffn1_w_down, mix_norm, w_in,
              mla_q_norm, mla_w_uq, mla_kv_norm, mla_w_ukv,
              diff_lambda_q1, diff_lambda_k1, diff_lambda_q2, diff_lambda_k2, diff_subln,
              pool_w, pool_b, pool_scale, w_branch, w_out,
              ffn2_norm, ffn2_w_gate, ffn2_w_up, ffn2_w_down, final_norm):
    b, s, d = x.shape
    split_idx = [int(v) for v in np.cumsum(IN_SPLITS)[:-1]]
    h = x
    for l in range(DEPTH):
        h = h + 0.5 * swiglu(rmsnorm(h, ffn1_norm[l]), ffn1_w_gate[l], ffn1_w_up[l], ffn1_w_down[l])

        u = rmsnorm(h, mix_norm[l])
        z = u @ w_in[l]
        c_q, c_kv, k_rope, dq, dk, dv, p_in, z_gate = jnp.split(z, split_idx, axis=-1)

        y_mla = mla_mixer(c_q, c_kv, k_rope, positions,
                          mla_q_norm[l], mla_w_uq[l], mla_kv_norm[l], mla_w_ukv[l])
        lambda_init = 0.8 - 0.6 * math.exp(-0.3 * l)
        y_diff = diff_mixer(dq, dk, dv, positions, diff_lambda_q1[l], diff_lambda_k1[l],
                            diff_lambda_q2[l], diff_lambda_k2[l], diff_subln[l], lambda_init)
        y_pool = pool_mixer(p_in, pool_w[l], pool_b[l], pool_scale[l])

        branches = jnp.stack([y_mla, y_diff, y_pool], axis=2)
        branches = jnp.einsum('bsnc,ncd->bsnd', branches, w_branch[l])
        gates = jax.nn.sigmoid(z_gate.reshape(b, s, N_BRANCH, d))
        merged = jnp.sum(gates * branches, axis=2)
        h = h + merged @ w_out[l]

        h = h + 0.5 * swiglu(rmsnorm(h, ffn2_norm[l]), ffn2_w_gate[l], ffn2_w_up[l], ffn2_w_down[l])
    return rmsnorm(h, final_norm)
```

```python
import math
import numpy as np
import concourse.bass as bass
import concourse.mybir as mybir
from concourse.bass_utils import run_bass_kernel_spmd

F32 = mybir.dt.float32
BF16 = mybir.dt.bfloat16
I32 = mybir.dt.int32
AF = mybir.ActivationFunctionType
ALU = mybir.AluOpType

NCORES = 8
D = 1024
DFF = 2816
DEPTH = 2
B = 2
S = 8192
T = 2048
TT = 512
NT = T // TT
KD = D // 128
EPS = 1e-6
INW = 5536
SC_MLA = 96.0 ** -0.5
SC_DIFF = 32.0 ** -0.5


class Res:
    __slots__ = ("name", "w", "r")

    def __init__(self, name):
        self.name = name
        self.w = None
        self.r = {}


class Slot:
    def __init__(self):
        self.sem = None
        self.count = 0
        self.kind = None


class Sched:
    ENG = ("pe", "act", "dve", "pool", "sp")

    def __init__(self, nc):
        self.nc = nc
        self.prog = {e: [] for e in self.ENG}
        self.esem = {e: nc.alloc_semaphore("S_" + e) for e in ("pe", "act", "dve", "pool")}
        self.ecnt = {e: 0 for e in self.esem}
        self.waited = {e: {} for e in self.ENG}
        self.nslots = 0
        self.slots = []
        self.customs = []
        self.free_sems = {"sp": [], "pool": []}
        self.semcount = {}
        self.scopes = []

    def slot(self, name=None):
        sl = Slot()
        if self.scopes:
            self.scopes[-1].append(sl)
        return sl

    def _bind(self, sl, q):
        if sl.sem is None:
            pool = self.free_sems[q]
            if pool:
                sl.sem, sl.count = pool.pop()
            else:
                self.nslots += 1
                sl.sem, sl.count = self.nc.alloc_semaphore("D%d" % self.nslots), 0
            sl.kind = q
        assert sl.kind == q, "a DMA semaphore must stay on one queue type"

    def scope_push(self):
        self.scopes.append([])

    def scope_pop(self):
        self.barrier()
        for sl in self.scopes.pop():
            if sl.sem is not None:
                self.free_sems[sl.kind].append((sl.sem, sl.count))
                sl.sem = None

    def _deps(self, eng, reads, writes):
        deps = {}

        def add(d):
            if d is None:
                return
            sem, val, deng = d
            k = id(sem)
            if k not in deps or deps[k][1] < val:
                deps[k] = (sem, val, deng)

        for r in reads:
            add(r.w)
        for w in writes:
            add(w.w)
            for d in w.r.values():
                add(d)
        for k, (sem, val, deng) in deps.items():
            if deng == eng and eng == "pe":
                continue
            if self.waited[eng].get(k, 0) >= val:
                continue
            self.waited[eng][k] = val
            self.prog[eng].append(("wait", sem, val))

    def _mark(self, tok, reads, writes):
        k = id(tok[0])
        for r in reads:
            r.r[k] = tok
        for w in writes:
            w.w = tok
            w.r = {}

    def op(self, eng, fn, reads=(), writes=()):
        self._deps(eng, reads, writes)
        self.ecnt[eng] += 1
        tok = (self.esem[eng], self.ecnt[eng], eng)
        self.prog[eng].append(("op", fn, self.esem[eng], 1))
        self._mark(tok, reads, writes)

    def dma(self, q, slot, fns, reads=(), writes=()):
        self._bind(slot, q)
        self._deps(q, reads, writes)
        for fn in fns:
            slot.count += 16
            self.prog[q].append(("op", fn, slot.sem, 16))
        self.semcount[id(slot.sem)] = (slot.sem, slot.count)
        tok = (slot.sem, slot.count, "dma")
        self._mark(tok, reads, writes)

    def custom(self, q, fn, sem, val, reads=(), writes=()):
        self._deps(q, reads, writes)
        self.prog[q].append(("opc", fn, sem))
        self.customs.append((sem, val))
        tok = (sem, val, "cc")
        self._mark(tok, reads, writes)

    def barrier(self):
        targets = []
        for e, sem in self.esem.items():
            if self.ecnt[e] > 0:
                targets.append((sem, self.ecnt[e]))
        for (sem, cnt_) in self.semcount.values():
            targets.append((sem, cnt_))
        for eng in self.ENG:
            for sem, val in targets:
                k = id(sem)
                if self.waited[eng].get(k, 0) >= val:
                    continue
                self.waited[eng][k] = val
                self.prog[eng].append(("wait", sem, val))

    def final_wait(self, eng, res_list):
        self._deps(eng, res_list, ())

    def emit(self, eng_name, e):
        for it in self.prog[eng_name]:
            if it[0] == "wait":
                e.wait_ge(it[1], it[2])
            elif it[0] == "op":
                ins = it[1](e)
                ins.then_inc(it[2], it[3])
            else:
                ins = it[1](e)
                ins.then_inc(it[2])


class Arena:
    def __init__(self, nc, nbytes):
        self.t = nc.alloc_sbuf_tensor("arena", [128, nbytes // 4], F32)
        self.nbytes = nbytes
        self.off = 0
        self.marks = []

    def alloc(self, free_shape, dtype):
        esz = 4 if dtype in (F32, I32) else 2
        n = int(np.prod(free_shape)) * esz
        n_al = (n + 63) // 64 * 64
        assert self.off + n_al <= self.nbytes, ("SBUF arena overflow", self.off, n_al, self.nbytes)
        a = self.t[:, self.off // 4:(self.off + n_al) // 4]
        self.off += n_al
        if dtype != F32:
            a = a.bitcast(dtype)
        a = a[:, 0:int(np.prod(free_shape))]
        if len(free_shape) == 2:
            a = a.rearrange("p (a b) -> p a b", a=free_shape[0])
        elif len(free_shape) == 3:
            a = a.rearrange("p (a b c) -> p a b c", a=free_shape[0], b=free_shape[1])
        elif len(free_shape) == 4:
            a = a.rearrange("p (a b c d) -> p a b c d", a=free_shape[0], b=free_shape[1], c=free_shape[2])
        return a

    def push(self):
        self.marks.append(self.off)
        if self.on_push is not None:
            self.on_push()

    on_push = None

    def pop(self):
        self.off = self.marks.pop()
        if self.on_pop is not None:
            self.on_pop()

    on_pop = None


class Prog:
    def __init__(self, stage=99):
        self.stage = stage
        nc = bass.Bass("TRN2", target_bir_lowering=False)
        self.nc = nc
        self.S = Sched(nc)
        self.A = Arena(nc, 200 * 1024)
        self.A.on_pop = self.S.scope_pop
        self.A.on_push = self.S.scope_push
        self.din = {}
        self.build()

    def inp(self, name, shape, dtype=F32):
        t = self.nc.dram_tensor(name, list(shape), dtype, kind="ExternalInput")
        self.din[name] = t
        return t

    def build(self):
        nc, S, A = self.nc, self.S, self.A
        st = self.stage
        xT = self.inp("xT", [D, T])
        gains = self.inp("gains", [128, 64])
        ptab = self.inp("ptab", [128, 32])
        ones_in = self.inp("ones", [128, 128])
        w = {}
        self.noffn = st in (1.5, 2.5, 2.7, 3, 4)
        if not self.noffn:
            for nm in ("ffn1_w_gate", "ffn1_w_up", "ffn2_w_gate", "ffn2_w_up"):
                w[nm] = self.inp(nm, [DEPTH, D, DFF])
            for nm in ("ffn1_w_down", "ffn2_w_down"):
                w[nm] = self.inp(nm, [DEPTH, DFF, D])
        if st >= 1.5:
            w["w_in"] = self.inp("w_in", [DEPTH, D, INW]) if st >= 2 else None
            if st >= 2:
                w["mla_w_uq"] = self.inp("mla_w_uq", [DEPTH, 256, 768])
                w["mla_w_ukv"] = self.inp("mla_w_ukv", [DEPTH, 128, 1024])
                w["pool_w"] = self.inp("pool_w", [DEPTH, 4, 128, 128])
                w["w_branch"] = self.inp("w_branch", [DEPTH, 3, 512, D])
                w["w_out"] = self.inp("w_out", [DEPTH, D, D])
            self.c_lam = self.inp("lam", [128, 8, 32])
            self.c_pos_own = self.inp("pos_own", [32, T], I32)
            self.c_pos_own16 = self.inp("pos_own16", [128, 16], I32)
            self.c_pos_g64 = self.inp("pos_g64", [128, 64], I32)
            self.c_flags = self.inp("flags", [4, 128, 64])
            self.c_tri = self.inp("tri", [128, 4, 512])
            self.c_ident = self.inp("ident", [128, 128])
            self.c_band = self.inp("band", [128, 3, 4, 128])
            self.c_tailsel = self.inp("tailsel", [128, 2, 4, 4, 128])
        self.w = w
        outT = nc.dram_tensor("outT", [D, T], F32, kind="ExternalOutput")
        self.dbg = {}

        self.h = A.alloc([KD, T], F32)
        self.hres = [[Res("h%d_%d" % (k, t)) for t in range(NT)] for k in range(KD)]
        self.gains = A.alloc([64], F32)
        self.gres = Res("gains")
        self.ptab = A.alloc([32], F32)
        self.ones = A.alloc([128], F32)
        self.ones_bf = A.alloc([128], BF16)
        self.onesres = Res("ones")
        self.psum = nc.alloc_psum_tensor("psum", [128, 8, 512], F32)
        self.pres = [Res("ps%d" % i) for i in range(8)]
        self.ld = S.slot("ld0")

        self.DMA("sp", self.ld, [(self.gains, gains[:, :]), (self.ones, ones_in[:, :]), (self.ptab, ptab[:, :])],
                 writes=[self.gres, self.onesres])
        self.CP("dve", self.ones_bf, self.ones, [self.onesres], [self.onesres])
        xv = xT.ap().rearrange("(k p) t -> p k t", p=128)
        for k in range(KD):
            self.DMA("sp", S.slot("ldx%d" % k), [(self.h[:, k, :], xv[:, k, :])], writes=self.hres[k])

        if st >= 1.5:
            self.setup_mixer()
            self.setup_mixer_compute()
            self._setup_done = True
        if st == 1.5:
            tabdbg = nc.dram_tensor("tabdbg", [2, 32, T], F32, kind="ExternalOutput")
            r = Res("tabdbg")
            self.DMA("sp", S.slot("tdbg"), [(tabdbg.ap()[0], self.cos_t[64:96]), (tabdbg.ap()[1], self.sin_t[64:96])],
                     reads=[self.r_rope], writes=[r])
            self.dbg_res = [r]

        for l in range(DEPTH):
            if st <= 0:
                break
            if not self.noffn:
                self.ffn(l, w["ffn1_w_gate"], w["ffn1_w_up"], w["ffn1_w_down"], gcol=0 + l * 8)
            if st <= 1.5:
                break
            if not self._setup_done:
                self.setup_mixer_compute()
                self._setup_done = True
            self.mixer(l)
            if st < 5:
                break
            self.ffn(l, w["ffn2_w_gate"], w["ffn2_w_up"], w["ffn2_w_down"], gcol=32 + l * 8)
            if st < 6:
                break
        if st >= 7:
            self.final_norm()

        ov = outT.ap().rearrange("(k p) t -> p k t", p=128)
        self.st = S.slot("st")
        allh = [r for row in self.hres for r in row]
        for k in range(KD):
            self.DMA("sp", self.st, [(ov[:, k, :], self.h[:, k, :])], reads=self.hres[k])
        S.final_wait("sp", allh + list(self.dbg_res))

        with nc.Block() as block:
            @block.tensor
            def _(e):
                S.emit("pe", e)

            @block.scalar
            def _(e):
                S.emit("act", e)

            @block.vector
            def _(e):
                S.emit("dve", e)

            @block.gpsimd
            def _(e):
                S.emit("pool", e)

            @block.sync
            def _(e):
                S.emit("sp", e)

    dbg_res = ()

    def final_norm(self):
        A = self.A
        A.push()
        sq = [A.alloc([TT], BF16) for _ in range(2)]
        sqres = [Res("fsq%d" % i) for i in range(2)]
        rs = [A.alloc([TT], F32) for _ in range(2)]
        rsres = [Res("frs%d" % i) for i in range(2)]
        self.rmsnorm_fm(self.h, self.hres, self.h, self.hres, 48, KD, D, sq, sqres, rs, rsres, stat_bank=7)
        A.pop()

    def ACT(self, out, in_, func, reads, writes, **kw):
        self.S.op("act", lambda e: e.activation(out=out, in_=in_, func=func, **kw), reads, writes)

    def MM(self, out, pairs, reads, writes, start=True, stop=True):
        n = len(pairs)

        def fn(e):
            ins = None
            for i, (l, r) in enumerate(pairs):
                ins = e.matmul(out, l, r, start=(start and i == 0), stop=(stop and i == n - 1))
            return ins
        self.S.op("pe", fn, reads, writes)

    def MMI(self, outs, pair_lists, reads, writes):
        n = len(pair_lists[0])

        def fn(e):
            ins = None
            for i in range(n):
                for o, pl in zip(outs, pair_lists):
                    l_, r_ = pl[i]
                    ins = e.matmul(o, l_, r_, start=(i == 0), stop=(i == n - 1))
            return ins
        self.S.op("pe", fn, reads, writes)

    def STT(self, eng, out, in0, scalar, in1, op0, op1, reads, writes):
        self.S.op(eng, lambda e: e.scalar_tensor_tensor(out, in0, scalar, in1, op0, op1), reads, writes)

    def TT(self, eng, out, in0, in1, op, reads, writes):
        self.S.op(eng, lambda e: e.tensor_tensor(out, in0, in1, op), reads, writes)

    def TS(self, eng, out, in0, s1, s2, op0, op1, reads, writes):
        if op1 is None:
            self.S.op(eng, lambda e: e.tensor_scalar(out, in0, s1, None, op0), reads, writes)
        else:
            self.S.op(eng, lambda e: e.tensor_scalar(out, in0, s1, s2, op0, op1), reads, writes)

    def CP(self, eng, out, in_, reads, writes):
        self.S.op(eng, lambda e: e.tensor_copy(out, in_), reads, writes)

    def RECIP(self, out, in_, reads, writes):
        self.S.op("dve", lambda e: e.reciprocal(out, in_), reads, writes)

    def MEMSET(self, eng, out, val, writes):
        self.S.op(eng, lambda e: e.memset(out, val), (), writes)

    def DMA(self, q, slot, pairs, reads=(), writes=()):
        fns = [(lambda e, o=o, i=i: e.dma_start(out=o, in_=i)) for (o, i) in pairs]
        self.S.dma(q, slot, fns, reads, writes)

    def rmsnorm_tile(self, srcs, sres, dsts, dres, gcols, width, sq, sqres, rs, rsres, stat_bank, npart=128,
                     gtab=None):
        ps = self.psum
        P = slice(0, npart)
        nk = len(srcs)
        for k in range(nk):
            b = self._sqi % len(sq)
            self._sqi += 1
            self.ACT(sq[b][P], srcs[k], AF.Square, [sres[k]], [sqres[b]])
            self.MM(ps[P, stat_bank, :], [(self.ones_bf[P, 0:npart], sq[b][P])], [sqres[b], self.onesres],
                    [self.pres[stat_bank]], start=(k == 0), stop=(k == nk - 1))
        r = self._rsi % len(rs)
        self._rsi += 1
        self.ACT(rs[r][P], ps[P, stat_bank, :], AF.Sqrt, [self.pres[stat_bank]], [rsres[r]],
                 scale=1.0 / width, bias=EPS)
        self.RECIP(rs[r][P], rs[r][P], [rsres[r]], [rsres[r]])
        for k in range(nk):
            self.STT("dve", dsts[k], srcs[k], gcols[k], rs[r][P], ALU.mult, ALU.mult,
                     [sres[k], rsres[r], self.gres], [dres[k]])

    _sqi = 0
    _rsi = 0

    def rmsnorm_fm(self, src, sres, dst, dres, gcol, nk, width, sq, sqres, rs, rsres, stat_bank, tiles=None):
        for t in (range(NT) if tiles is None else tiles):
            tsl = slice(t * TT, (t + 1) * TT)
            self.rmsnorm_tile([src[:, k, tsl] for k in range(nk)], [sres[k][t] for k in range(nk)],
                              [dst[:, k, tsl] for k in range(nk)], [dres[k][t] for k in range(nk)],
                              [self.gains[:, gcol + k:gcol + k + 1] for k in range(nk)], width,
                              sq, sqres, rs, rsres, stat_bank)

    def ffn(self, l, wg_d, wu_d, wd_d, gcol):
        S, A = self.S, self.A
        ps = self.psum
        A.push()
        G = 2
        NG = DFF // (128 * G)
        xn = A.alloc([KD, T], BF16)
        xres = [[Res("xn%d_%d" % (k, t)) for t in range(NT)] for k in range(KD)]
        sq = [A.alloc([TT], BF16) for _ in range(4)]
        sqres = [Res("sq%d" % i) for i in range(4)]
        rs = [A.alloc([TT], F32) for _ in range(2)]
        rsres = [Res("rs%d" % i) for i in range(2)]
        NS = 3
        wg = [A.alloc([KD, 128 * G], BF16) for _ in range(NS)]
        wu = [A.alloc([KD, 128 * G], BF16) for _ in range(NS)]
        wd = [A.alloc([G, D], BF16) for _ in range(NS)]
        wres = [Res("w%d" % i) for i in range(NS)]
        wslot = [S.slot() for _ in range(NS)]
        hid = [A.alloc([G, T], BF16) for _ in range(2)]
        hidres = [[[Res("hid") for t in range(NT)] for c in range(G)] for _ in range(2)]
        sil = [A.alloc([TT], F32) for _ in range(2)]
        silres = [Res("sil%d" % i) for i in range(2)]

        self.rmsnorm_fm(self.h, self.hres, xn, xres, gcol, KD, D, sq, sqres, rs, rsres, stat_bank=7)
        if self.stage == 0.5:
            for k in range(KD):
                for t in range(NT):
                    S.op("dve", lambda e, k=k, t=t: e.tensor_copy(self.h[:, k, t * TT:(t + 1) * TT], xn[:, k, t * TT:(t + 1) * TT]),
                         reads=[xres[k][t]], writes=[self.hres[k][t]])
            A.pop()
            return

        wgv = wg_d.ap()[l].rearrange("(k p) c -> p k c", p=128)
        wuv = wu_d.ap()[l].rearrange("(k p) c -> p k c", p=128)
        wdv = wd_d.ap()[l].rearrange("(c p) d -> p c d", p=128)

        def load(gi):
            s = gi % NS
            c0 = gi * 128 * G
            self.DMA("pool", wslot[s], [(wg[s], wgv[:, :, c0:c0 + 128 * G]),
                                        (wu[s], wuv[:, :, c0:c0 + 128 * G]),
                                        (wd[s], wdv[:, gi * G:(gi + 1) * G, :])], writes=[wres[s]])

        cnt = {"gu": 0, "dn": 0, "sil": 0}

        def up(gi):
            s = gi % NS
            hs = gi % 2
            for c in range(G):
                for t in range(NT):
                    tsl = slice(t * TT, (t + 1) * TT)
                    gb = cnt["gu"] % 2
                    ub = 2 + cnt["gu"] % 2
                    cnt["gu"] += 1
                    xr = [xres[k][t] for k in range(KD)]
                    csl = slice(c * 128, (c + 1) * 128)
                    self.MMI([ps[:, gb, :], ps[:, ub, :]],
                             [[(wg[s][:, k, csl], xn[:, k, tsl]) for k in range(KD)],
                              [(wu[s][:, k, csl], xn[:, k, tsl]) for k in range(KD)]],
                             xr + [wres[s]], [self.pres[gb], self.pres[ub]])
                    sb = cnt["sil"] % 2
                    cnt["sil"] += 1
                    self.ACT(sil[sb], ps[:, gb, :], AF.Silu, [self.pres[gb]], [silres[sb]])
                    self.TT("dve", hid[hs][:, c, tsl], ps[:, ub, :], sil[sb], ALU.mult,
                            [self.pres[ub], silres[sb]], [hidres[hs][c][t]])

        def down(gi):
            s = gi % NS
            hs = gi % 2
            for t in range(NT):
                tsl = slice(t * TT, (t + 1) * TT)
                for d in range(KD):
                    db = 4 + cnt["dn"] % 3
                    cnt["dn"] += 1
                    dsl = slice(d * 128, (d + 1) * 128)
                    self.MM(ps[:, db, :], [(wd[s][:, c, dsl], hid[hs][:, c, tsl]) for c in range(G)],
                            [hidres[hs][c][t] for c in range(G)] + [wres[s]], [self.pres[db]])
                    self.STT("dve", self.h[:, d, tsl], ps[:, db, :], 0.5, self.h[:, d, tsl], ALU.mult, ALU.add,
                             [self.pres[db], self.hres[d][t]], [self.hres[d][t]])

        load(0)
        load(1)
        up(0)
        for gi in range(NG):
            if gi + 2 < NG:
                load(gi + 2)
            if gi + 1 < NG:
                up(gi + 1)
            down(gi)
        A.pop()


    def dv(self, t, off, *dims):
        return bass.AP(t, off, [[a, b] for (a, b) in dims])

    def bank(self, lo=0, hi=6):
        key = (lo, hi)
        c = self._bk.get(key, 0)
        self._bk[key] = c + 1
        return lo + c % (hi - lo)

    _bk = {}

    def setup_mixer(self):
        nc, S, A = self.nc, self.S, self.A
        self._bk = {}
        KSZ, VSZ = 64 * 2048, 128 * 1040
        self.KSZ, self.VSZ = KSZ, VSZ
        self.gch = []
        self.gloc = {}

        def add_chunk(entries):
            n = sum(sz for (_, sz) in entries)
            assert n % 512 == 0 and n // 512 <= 1024
            ci = len(self.gch)
            gi = nc.dram_tensor("gin%d" % ci, [n // 512, 512], BF16)
            go = nc.dram_tensor("gout%d" % ci, [4 * (n // 512), 512], BF16)
            self.gch.append((gi, go, n))
            o = 0
            for key, sz in entries:
                self.gloc[key] = (ci, o)
                o += sz
        for sec in ("dK", "mK"):
            for h0 in (0, 4):
                add_chunk([((sec, h), KSZ) for h in range(h0, h0 + 4)])
        self.vgroups = [(0, 3), (3, 6), (6, 8)]
        for sec in ("dV", "mV"):
            for (h0, h1) in self.vgroups:
                add_chunk([((sec, h), VSZ) for h in range(h0, h1)])
        add_chunk([(("mR", 0), 32 * 2048), (("tl", 0), 64 * 512)])
        self.rg_in = [Res("gin%d" % i) for i in range(len(self.gch))]
        self.rg_out = [Res("gout%d" % i) for i in range(len(self.gch))]
        self.mla_chunks = [self.gloc[k][0] for k in (("mR", 0), ("mK", 0), ("mV", 0), ("mV", 3), ("mK", 4), ("mV", 6))]
        self.diff_chunks = [self.gloc[k][0] for k in (("dK", 0), ("dV", 0), ("dV", 3), ("dK", 4), ("dV", 6))]
        dbgk = "ExternalOutput" if self.stage in (1.5, 2.5, 2.7, 3) else "Internal"
        self.q_mla_d = nc.dram_tensor("q_mla_d", [8, 128, 2048], BF16, kind=dbgk)
        self.q_diff_d = nc.dram_tensor("q_diff_d", [8, 128, 2048], BF16, kind=dbgk)
        self.o_d = nc.dram_tensor("o_d", [3, 512, 2048], BF16, kind=dbgk)
        self.p_d = nc.dram_tensor("p_d", [16, 128, 512], BF16, kind=dbgk)
        self.kaug_d = nc.dram_tensor("kaug_d", [2, 32, 8192], BF16, kind=dbgk)
        self.kaugo_d = nc.dram_tensor("kaugo_d", [2, 32, 2048], BF16, kind=dbgk)
        if dbgk == "ExternalOutput":
            self.gdbg = nc.dram_tensor("gdbg", [1024, 512], BF16, kind="ExternalOutput")
        self.r_gin = Res("gin")
        self.r_gout = Res("gout")
        self.r_qm = Res("qm")
        self.r_qd = Res("qd")
        self.r_od = Res("od")
        self.r_pd = Res("pd")
        self.r_kaug = Res("kaug")
        self.cc_sems = [[nc.alloc_semaphore("cc%d_%d" % (l, i)) for i in range(len(self.gch))] for l in range(DEPTH)]
        self.ccdummy = A.alloc([16], F32)

        self.cos_t = A.alloc([T], F32)
        self.sin_t = A.alloc([T], F32)
        self.r_rope = Res("rope")
        self.tri = A.alloc([4, 512], BF16)
        self.ident = A.alloc([128], BF16)
        self.r_tri = Res("tri")
        self.tab2 = A.alloc([16], F32)
        self.r_tab2 = Res("tab2")
        self.cslot = S.slot("cslot")
        self.DMA("pool", self.cslot, [(self.tri, self.c_tri[:, :, :]), (self.ident, self.c_ident[:, :])],
                 writes=[self.r_tri])

    def setup_mixer_compute(self):
        nc, S, A = self.nc, self.S, self.A
        A.push()
        P32 = slice(64, 96)
        posi = A.alloc([T], I32)
        ang = A.alloc([T], F32)
        kf = A.alloc([T], F32)
        ki = A.alloc([T], I32)
        r_t = Res("ropetmp")
        self.DMA("sp", S.slot("ldpos"), [(posi[P32], self.c_pos_own[:, :])], writes=[r_t])
        R = [r_t, self.gres]
        TWO_PI = 2.0 * math.pi

        def fold(x):
            self.TS("dve", kf[P32], x, math.pi, -TWO_PI, ALU.is_gt, ALU.mult, R, R)
            self.TT("dve", x, x, kf[P32], ALU.add, R, R)
            self.TS("dve", x, x, -3.14159, 3.14159, ALU.max, ALU.min, R, R)

        self.CP("dve", ang[P32], posi[P32], R, R)
        self.TS("dve", ang[P32], ang[P32], self.ptab[P32, 16:17], None, ALU.mult, None, R, R)
        self.TS("dve", kf[P32], ang[P32], 1.0 / TWO_PI, None, ALU.mult, None, R, R)
        self.CP("dve", ki[P32], kf[P32], R, R)
        self.CP("dve", kf[P32], ki[P32], R, R)
        self.STT("dve", ang[P32], kf[P32], -6.28125, ang[P32], ALU.mult, ALU.add, R, R)
        self.STT("dve", ang[P32], kf[P32], -0.0019353071795864769, ang[P32], ALU.mult, ALU.add, R, R)
        fold(ang[P32])
        self.ACT(self.sin_t[P32], ang[P32], AF.Sin, R, [self.r_rope])
        self.TS("dve", self.sin_t[P32], self.sin_t[P32], self.ptab[P32, 17:18], None, ALU.mult, None,
                [self.r_rope], [self.r_rope])
        self.TS("dve", ang[P32], ang[P32], math.pi / 2, None, ALU.add, None, R, R)
        fold(ang[P32])
        self.ACT(self.cos_t[P32], ang[P32], AF.Sin, R + [self.r_rope], [self.r_rope])
        A.pop()

        A.push()

        def digits(pos_in, n, tag):
            pi_ = A.alloc([n], I32)
            pf = A.alloc([n], F32)
            af = A.alloc([n], F32)
            bf = A.alloc([n], F32)
            mf = A.alloc([n], F32)
            ai = A.alloc([n], I32)
            rr = [Res("dig" + tag)]
            self.DMA("sp", S.slot("lddig" + tag), [(pi_, pos_in[:, :])], writes=rr)
            self.CP("dve", pf, pi_, rr, rr)
            self.TS("dve", af, pf, 1.0 / 128, None, ALU.mult, None, rr, rr)
            self.CP("dve", ai, af, rr, rr)
            self.CP("dve", af, ai, rr, rr)
            self.STT("dve", bf, af, -128.0, pf, ALU.mult, ALU.add, rr, rr)
            self.TS("dve", mf, bf, 0.0, None, ALU.is_lt, None, rr, rr)
            self.TT("dve", af, af, mf, ALU.subtract, rr, rr)
            self.STT("dve", bf, mf, 128.0, bf, ALU.mult, ALU.add, rr, rr)
            return af, bf, rr

        ak, bk, rk = digits(self.c_pos_g64, 64, "k")
        ao, bo, ro = digits(self.c_pos_own16, 16, "o")
        fl = A.alloc([4, 64], F32)
        r_fl = Res("fl")
        self.DMA("sp", S.slot("ldfl"), [(fl[:, j, :], self.c_flags[j]) for j in range(4)], writes=[r_fl])

        KA = A.alloc([32, 64], BF16)
        KAo = A.alloc([32, 16], BF16)
        r_ka = Res("KA")
        kslot = S.slot("kslot")
        for v in range(2):
            self.MEMSET("dve", KA, 0.0, [r_ka])
            self.MEMSET("dve", KAo, 0.0, [r_ka])
            if v == 0:
                self.CP("dve", KA[:, 0, :], ak, rk + [r_ka], [r_ka])
                self.CP("dve", KA[:, 1, :], bk, rk + [r_ka], [r_ka])
                self.MEMSET("dve", KA[:, 2:4, :], 1.0, [r_ka])
                self.CP("dve", KAo[:, 0, :], ao, ro + [r_ka], [r_ka])
                self.CP("dve", KAo[:, 1, :], bo, ro + [r_ka], [r_ka])
                self.MEMSET("dve", KAo[:, 2:4, :], 1.0, [r_ka])
                f0 = 4
            else:
                f0 = 0
            self.CP("dve", KA[:, f0:f0 + 4, :], fl, [r_fl, r_ka], [r_ka])
            self.DMA("sp", kslot, [
                (self.dv(self.kaug_d, v * 32 * 8192, (64, 128), (8192, 32), (1, 64)), KA),
                (self.dv(self.kaugo_d, v * 32 * 2048, (16, 128), (2048, 32), (1, 16)), KAo)],
                reads=[r_ka, self.r_kaug])

        QA = A.alloc([32, 16], BF16)
        r_qa = Res("QA")
        qslot = S.slot("qslot")
        tmpq = A.alloc([16], F32)
        for hh in range(9):
            self.MEMSET("dve", QA, 0.0, [r_qa])
            f0 = 4 if hh < 8 else 0
            for j in range(4):
                self.MEMSET("dve", QA[32 * j:32 * j + 32, f0 + j, :], -30000.0, [r_qa])
            if hh < 8:
                sl = 2.0 ** (-(hh + 1))
                self.MEMSET("dve", QA[:, 0, :], 128.0 * sl, [r_qa])
                self.MEMSET("dve", QA[:, 1, :], sl, [r_qa])
                self.TS("dve", QA[:, 2, :], ao, -128.0 * sl, None, ALU.mult, None, ro + [r_qa], [r_qa])
                self.TS("dve", QA[:, 3, :], bo, -sl, None, ALU.mult, None, ro + [r_qa], [r_qa])
                self.DMA("sp", qslot, [
                    (self.dv(self.q_diff_d, hh * 128 * 2048 + r0 * 2048, (16, 128), (2048, 32), (1, 16)), QA)
                    for r0 in (0, 96)], reads=[r_qa, self.r_qd])
            else:
                self.DMA("sp", qslot, [
                    (self.dv(self.q_mla_d, h2 * 128 * 2048 + 96 * 2048, (16, 128), (2048, 32), (1, 16)), QA)
                    for h2 in range(8)], reads=[r_qa, self.r_qm])

        lam = A.alloc([8, 32], F32)
        r_lam = Res("lam")
        self.DMA("sp", S.slot("ldlam"), [(lam, self.c_lam[:, :, :])], writes=[r_lam])
        prod = A.alloc([32], F32)
        e12 = A.alloc([4], F32)
        for l in range(DEPTH):
            lam_init = 0.8 - 0.6 * math.exp(-0.3 * l)
            for i in range(2):
                self.TT("dve", prod, lam[:, l * 4 + 2 * i, :], lam[:, l * 4 + 2 * i + 1, :], ALU.mult, [r_lam], [r_lam])
                self.S.op("dve", (lambda e, o=e12[:, i:i + 1], p=prod: e.reduce_sum(o, p, mybir.AxisListType.X)),
                          [r_lam], [r_lam])
                self.ACT(e12[:, i:i + 1], e12[:, i:i + 1], AF.Exp, [r_lam], [r_lam])
            self.TT("dve", e12[:, 2:3], e12[:, 1:2], e12[:, 0:1], ALU.subtract, [r_lam], [r_lam])
            self.TS("dve", self.tab2[:, l:l + 1], e12[:, 2:3], -lam_init, None, ALU.add, None, [r_lam], [self.r_tab2])
            self.TS("dve", self.tab2[:, 2 + l:3 + l], self.gains[:, 62 + l:63 + l], 1.0 - lam_init, None, ALU.mult, None,
                    [self.gres], [self.r_tab2])
        A.pop()

    def mixer(self, l):
        self.phaseA(l)
        self.phaseB(l)
        if self.stage == 3:
            r = Res("dbg3")
            self.DMA("sp", self.S.slot("dbg3"), [(self.gdbg.ap()[0:8, :], self.gch[0][0].ap()[0:8, :])],
                     writes=[self.r_qm, self.r_qd, self.r_od, self.r_pd, self.r_kaug, r])
            self.dbg_res = [r]
            return
        self.phaseC0(l)
        self.phaseC1(l)

    def gin_ap(self, sec, h, off, *dims):
        ci, o = self.gloc[(sec, h)]
        return self.dv(self.gch[ci][0], o + off, *dims)

    def gout_ap(self, r, sec, h, off, *dims):
        ci, o = self.gloc[(sec, h)]
        return self.dv(self.gch[ci][1], r * self.gch[ci][2] + o + off, *dims)

    def gather_chunks(self, l, chunks):
        groups = [[0, 1, 2, 3], [4, 5, 6, 7]]
        for i in chunks:
            gi, go, n = self.gch[i]
            self.S.custom("pool", (lambda e, gi=gi, go=go: e.collective_compute(
                "AllGather", ALU.bypass, replica_groups=groups, ins=[gi.ap().opt()], outs=[go.ap().opt()])),
                self.cc_sems[l][i], 1, reads=[], writes=[self.rg_in[i], self.rg_out[i]])

    def ci(self, sec, h=0):
        return self.gloc[(sec, h)][0]

    def phaseA(self, l):
        A, S = self.A, self.S
        ps = self.psum
        A.push()
        u = A.alloc([KD, T], BF16)
        ures = [[Res("u") for t in range(NT)] for k in range(KD)]
        sq = [A.alloc([TT], BF16) for _ in range(2)]
        sqres = [Res("sq") for _ in range(2)]
        rs = [A.alloc([TT], F32) for _ in range(2)]
        rsres = [Res("rs") for _ in range(2)]
        self.rmsnorm_fm(self.h, self.hres, u, ures, 16 + l * 8, KD, D, sq, sqres, rs, rsres, stat_bank=7)

        win = self.w["w_in"].ap()[l].rearrange("(k p) c -> p k c", p=128)
        NW = 2
        wb = [A.alloc([KD, 576], BF16) for _ in range(NW)]
        wbres = [Res("wb") for _ in range(NW)]
        wbslot = [S.slot() for _ in range(NW)]
        for i in range(NW):
            self.MEMSET("pool", wb[i][:, :, 544:576], 0.0, [wbres[i]])
        self._wbi = 0

        def wload(pairs_fn):
            i = self._wbi % NW
            self._wbi += 1
            self.DMA("pool", wbslot[i], pairs_fn(wb[i]), writes=[wbres[i]])
            return wb[i], wbres[i]

        uq = A.alloc([2, 8, 96], BF16)
        uqr = A.alloc([2, 8, 96], BF16)
        ukvk = A.alloc([8, 64], BF16)
        ukvv = A.alloc([8, 64], BF16)
        r_sw = Res("smallw")
        swslot = S.slot()
        uqv = self.w["mla_w_uq"].ap()[l].rearrange("(k p) (h c) -> p k h c", p=128, c=96)
        ukvv_d = self.w["mla_w_ukv"].ap()[l].rearrange("p (h c) -> p h c", c=128)
        self.MEMSET("pool", uqr, 0.0, [r_sw])
        swp = [(ukvk, ukvv_d[:, :, 0:64]), (ukvv, ukvv_d[:, :, 64:128])]
        for k2 in range(2):
            swp += [(uq[:, k2, :, :], uqv[:, k2, :, :]),
                    (uqr[:, k2, :, 64:80], uqv[:, k2, :, 80:96]), (uqr[:, k2, :, 80:96], uqv[:, k2, :, 64:80])]
        self.DMA("pool", swslot, swp, writes=[r_sw])

        stg = [A.alloc([8, TT], BF16) for _ in range(2)]
        stgres = [Res("stg") for _ in range(2)]
        stgslot = [S.slot() for _ in range(2)]
        vstg = [A.alloc([8, 4, 65], BF16) for _ in range(2)]
        vstgres = [Res("vstg") for _ in range(2)]
        vstgslot = [S.slot() for _ in range(2)]
        for i in range(2):
            self.MEMSET("pool", vstg[i][:, :, :, 64:65], 1.0, [vstgres[i]])
        pstg = [A.alloc([4, 512], BF16) for _ in range(2)]
        pstgres = [Res("pstg") for _ in range(2)]
        pstgslot = [S.slot() for _ in range(2)]
        cq = A.alloc([2, TT], F32)
        cqn = A.alloc([2, TT], BF16)
        r_cq = [Res("cq0"), Res("cq1")]
        r_cqn = [Res("cqn0"), Res("cqn1")]
        ckv = A.alloc([TT], F32)
        ckvn = A.alloc([TT], BF16)
        r_ckv, r_ckvn = Res("ckv"), Res("ckvn")
        t1 = A.alloc([TT], F32)
        t2 = A.alloc([TT], F32)
        r_t1, r_t2 = Res("t1"), Res("t2")
        rstg = A.alloc([TT], BF16)
        r_rstg = Res("rstg")
        rslot = S.slot()
        cnt = {"stg": 0, "v": 0, "p": 0}
        P32 = slice(64, 96)

        def urow(t):
            return [ures[k][t] for k in range(KD)]

        def rope(dst, x_ps, xs_ps, tsl, scale, reads, wres_):
            self.STT("dve", t1[P32], x_ps, scale, self.cos_t[P32, tsl], ALU.mult, ALU.mult,
                     reads + [self.r_rope], [r_t1])
            self.STT("dve", t2[P32], xs_ps, scale, self.sin_t[P32, tsl], ALU.mult, ALU.mult,
                     reads + [self.r_rope], [r_t2])
            self.TT("dve", dst, t1[P32], t2[P32], ALU.add, [r_t1, r_t2], wres_)

        wq, rwq = wload(lambda b: [(b[:, :, 0:256], win[:, :, 0:256])])
        for t in range(NT):
            tsl = slice(t * TT, (t + 1) * TT)
            for c2 in range(2):
                bk = self.bank()
                self.MM(ps[:, bk, :], [(wq[:, k, c2 * 128:(c2 + 1) * 128], u[:, k, tsl]) for k in range(KD)],
                        urow(t) + [rwq], [self.pres[bk]])
                self.ACT(cq[:, c2, :], ps[:, bk, :], AF.Copy, [self.pres[bk]], [r_cq[c2]])
            self.rmsnorm_tile([cq[:, 0, :], cq[:, 1, :]], r_cq, [cqn[:, 0, :], cqn[:, 1, :]], r_cqn,
                              [self.gains[:, 56 + 2 * l + c2:57 + 2 * l + c2] for c2 in range(2)], 256,
                              sq, sqres, rs, rsres, stat_bank=7)
            si = cnt["stg"] % 2
            cnt["stg"] += 1
            for hh in range(8):
                b1 = self.bank()
                b2 = self.bank()
                self.MM(ps[0:96, b1, :], [(uq[:, k2, hh, :], cqn[:, k2, :]) for k2 in range(2)],
                        r_cqn + [r_sw], [self.pres[b1]])
                self.MM(ps[0:96, b2, :], [(uqr[:, k2, hh, :], cqn[:, k2, :]) for k2 in range(2)],
                        r_cqn + [r_sw], [self.pres[b2]])
                self.ACT(stg[si][0:64, hh, :], ps[0:64, b1, :], AF.Copy, [self.pres[b1]], [stgres[si]],
                         scale=SC_MLA)
                rope(stg[si][64:96, hh, :], ps[64:96, b1, :], ps[64:96, b2, :], tsl, SC_MLA,
                     [self.pres[b1], self.pres[b2]], [stgres[si]])
            self.DMA("sp", stgslot[si], [(self.dv(self.q_mla_d, t * TT, (2048, 96), (128 * 2048, 8), (1, TT)),
                                          stg[si][0:96, :, :])], reads=[stgres[si], self.r_qm])

        wk, rwk = wload(lambda b: [(b[:, :, 0:160], win[:, :, 256:416]), (b[:, :, 160:176], win[:, :, 400:416]),
                                   (b[:, :, 176:192], win[:, :, 384:400])])
        for t in range(NT):
            tsl = slice(t * TT, (t + 1) * TT)
            bk = self.bank()
            self.MM(ps[:, bk, :], [(wk[:, k, 0:128], u[:, k, tsl]) for k in range(KD)], urow(t) + [rwk],
                    [self.pres[bk]])
            self.ACT(ckv, ps[:, bk, :], AF.Copy, [self.pres[bk]], [r_ckv])
            self.rmsnorm_tile([ckv], [r_ckv], [ckvn], [r_ckvn], [self.gains[:, 60 + l:61 + l]], 128,
                              sq, sqres, rs, rsres, stat_bank=7)
            si = cnt["stg"] % 2
            cnt["stg"] += 1
            for hh in range(8):
                bk = self.bank()
                self.MM(ps[0:64, bk, :], [(ukvk[:, hh, :], ckvn)], [r_ckvn, r_sw], [self.pres[bk]])
                if hh % 2:
                    self.CP("dve", stg[si][0:64, hh, :], ps[0:64, bk, :], [self.pres[bk]], [stgres[si]])
                else:
                    self.ACT(stg[si][0:64, hh, :], ps[0:64, bk, :], AF.Copy, [self.pres[bk]], [stgres[si]])
            self.DMA("sp", stgslot[si], [(self.gin_ap("mK", h0, t * TT, (2048, 64), (64 * 2048, 4), (1, TT)),
                                          stg[si][0:64, h0:h0 + 4, :]) for h0 in (0, 4)],
                     reads=[stgres[si], self.rg_in[self.ci("mK", 0)], self.rg_in[self.ci("mK", 4)]])
            vi = cnt["v"] % 2
            cnt["v"] += 1
            for blk in range(4):
                bk = self.bank()
                self.MM(ps[:, bk, :], [(ckvn[:, blk * 128:(blk + 1) * 128], ukvv.rearrange("p h c -> p (h c)"))],
                        [r_ckvn, r_sw], [self.pres[bk]])
                self.CP("dve", vstg[vi][:, :, blk, 0:64], ps[:, bk, :].rearrange("p (h c) -> p h c", c=64),
                        [self.pres[bk]], [vstgres[vi]])
            self.DMA("sp", vstgslot[vi], [(self.gin_ap("mV", h0, t * 260, (1040, 128), (128 * 1040, h1 - h0), (1, 260)),
                                           vstg[vi][:, h0:h1].rearrange("p h b c -> p h (b c)"))
                                          for (h0, h1) in self.vgroups],
                     reads=[vstgres[vi]] + [self.rg_in[self.ci("mV", h0)] for (h0, h1) in self.vgroups])
            b1 = self.bank()
            b2 = self.bank()
            self.MM(ps[0:96, b1, :], [(wk[:, k, 64:160], u[:, k, tsl]) for k in range(KD)], urow(t) + [rwk],
                    [self.pres[b1]])
            self.MM(ps[0:96, b2, :], [(wk[:, k, 96:192], u[:, k, tsl]) for k in range(KD)], urow(t) + [rwk],
                    [self.pres[b2]])
            rope(rstg[P32], ps[64:96, b1, :], ps[64:96, b2, :], tsl, 1.0, [self.pres[b1], self.pres[b2]], [r_rstg])
            self.DMA("sp", rslot, [(self.gin_ap("mR", 0, t * TT, (2048, 32), (1, TT)), rstg[P32])],
                     reads=[r_rstg, self.rg_in[self.ci("mR")]])

        for which in (1,):
            c0 = 1440 + which * 512
            wv_, rwv = wload(lambda b, c0=c0: [(b[:, :, 0:512], win[:, :, c0:c0 + 512])])
            for t in range(NT):
                if which == 0:
                    vi = cnt["v"] % 2
                    cnt["v"] += 1
                else:
                    pi_ = cnt["p"] % 2
                    cnt["p"] += 1
                for blk in range(4):
                    tok = slice(t * TT + blk * 128, t * TT + (blk + 1) * 128)
                    bk = self.bank()
                    self.MM(ps[:, bk, :], [(u[:, k, tok], wv_[:, k, 0:512]) for k in range(KD)], urow(t) + [rwv],
                            [self.pres[bk]])
                    if which == 0:
                        pv3 = ps[:, bk, :].rearrange("p (h c) -> p h c", c=64)
                        if blk % 2:
                            self.CP("dve", vstg[vi][:, :, blk, 0:64], pv3, [self.pres[bk]], [vstgres[vi]])
                        else:
                            self.ACT(vstg[vi][:, :, blk, 0:64], pv3, AF.Copy, [self.pres[bk]], [vstgres[vi]])
                    else:
                        if blk % 2:
                            self.CP("dve", pstg[pi_][:, blk, :], ps[:, bk, :], [self.pres[bk]], [pstgres[pi_]])
                        else:
                            self.ACT(pstg[pi_][:, blk, :], ps[:, bk, :], AF.Copy, [self.pres[bk]], [pstgres[pi_]])
                if which == 0:
                    self.DMA("sp", vstgslot[vi], [(self.gin_ap("dV", h0, t * 260, (1040, 128), (128 * 1040, h1 - h0),
                                                               (1, 260)),
                                                   vstg[vi][:, h0:h1].rearrange("p h b c -> p h (b c)"))
                                                  for (h0, h1) in self.vgroups],
                             reads=[vstgres[vi]] + [self.rg_in[self.ci("dV", h0)] for (h0, h1) in self.vgroups])
                else:
                    self.DMA("sp", pstgslot[pi_], [
                        (self.dv(self.p_d, t * 4 * 128 * 512, (512, 128), (128 * 512, 4), (1, 512)), pstg[pi_]),
                        (self.gin_ap("tl", 0, t * 16 * 512, (512, 16), (1, 512)), pstg[pi_][112:128, 3, :])],
                        reads=[pstgres[pi_], self.rg_in[self.ci("tl")], self.r_pd])
        self.gather_chunks(l, self.mla_chunks)
        for which in range(2):
            c0 = 416 + which * 512
            wd_, rwd = wload(lambda b, c0=c0: [(b[:, :, 32:544], win[:, :, c0:c0 + 512])])
            for t in range(NT):
                tsl = slice(t * TT, (t + 1) * TT)
                si = cnt["stg"] % 2
                cnt["stg"] += 1
                for hh in range(8):
                    bk = self.bank()
                    self.MM(ps[:, bk, :], [(wd_[:, k, hh * 64:hh * 64 + 128], u[:, k, tsl]) for k in range(KD)],
                            urow(t) + [rwd], [self.pres[bk]])
                    sc_ = SC_DIFF if which == 0 else 1.0
                    self.TS("dve", stg[si][32:64, hh, :], ps[32:64, bk, :], sc_, None, ALU.mult, None,
                            [self.pres[bk]], [stgres[si]])
                    self.ACT(stg[si][64:96, hh, :], ps[64:96, bk, :], AF.Copy, [self.pres[bk]],
                             [stgres[si]], scale=sc_)
                if which == 0:
                    self.DMA("sp", stgslot[si], [(self.dv(self.q_diff_d, 32 * 2048 + t * TT, (2048, 64), (128 * 2048, 8),
                                                          (1, TT)), stg[si][32:96, :, :])],
                             reads=[stgres[si], self.r_qd])
                else:
                    self.DMA("sp", stgslot[si], [(self.gin_ap("dK", h0, t * TT, (2048, 64), (64 * 2048, 4), (1, TT)),
                                                  stg[si][32:96, h0:h0 + 4, :]) for h0 in (0, 4)],
                             reads=[stgres[si], self.rg_in[self.ci("dK", 0)], self.rg_in[self.ci("dK", 4)]])

        for which in (0,):
            c0 = 1440 + which * 512
            wv_, rwv = wload(lambda b, c0=c0: [(b[:, :, 0:512], win[:, :, c0:c0 + 512])])
            for t in range(NT):
                if which == 0:
                    vi = cnt["v"] % 2
                    cnt["v"] += 1
                else:
                    pi_ = cnt["p"] % 2
                    cnt["p"] += 1
                for blk in range(4):
                    tok = slice(t * TT + blk * 128, t * TT + (blk + 1) * 128)
                    bk = self.bank()
                    self.MM(ps[:, bk, :], [(u[:, k, tok], wv_[:, k, 0:512]) for k in range(KD)], urow(t) + [rwv],
                            [self.pres[bk]])
                    if which == 0:
                        pv3 = ps[:, bk, :].rearrange("p (h c) -> p h c", c=64)
                        if blk % 2:
                            self.CP("dve", vstg[vi][:, :, blk, 0:64], pv3, [self.pres[bk]], [vstgres[vi]])
                        else:
                            self.ACT(vstg[vi][:, :, blk, 0:64], pv3, AF.Copy, [self.pres[bk]], [vstgres[vi]])
                    else:
                        if blk % 2:
                            self.CP("dve", pstg[pi_][:, blk, :], ps[:, bk, :], [self.pres[bk]], [pstgres[pi_]])
                        else:
                            self.ACT(pstg[pi_][:, blk, :], ps[:, bk, :], AF.Copy, [self.pres[bk]], [pstgres[pi_]])
                if which == 0:
                    self.DMA("sp", vstgslot[vi], [(self.gin_ap("dV", h0, t * 260, (1040, 128), (128 * 1040, h1 - h0),
                                                               (1, 260)),
                                                   vstg[vi][:, h0:h1].rearrange("p h b c -> p h (b c)"))
                                                  for (h0, h1) in self.vgroups],
                             reads=[vstgres[vi]] + [self.rg_in[self.ci("dV", h0)] for (h0, h1) in self.vgroups])
                else:
                    self.DMA("sp", pstgslot[pi_], [
                        (self.dv(self.p_d, t * 4 * 128 * 512, (512, 128), (128 * 512, 4), (1, 512)), pstg[pi_]),
                        (self.gin_ap("tl", 0, t * 16 * 512, (512, 16), (1, 512)), pstg[pi_][112:128, 3, :])],
                        reads=[pstgres[pi_], self.rg_in[self.ci("tl")], self.r_pd])
        self.gather_chunks(l, self.diff_chunks)
        A.pop()


    def phaseB(self, l):
        A, S = self.A, self.S
        ps = self.psum
        A.push()
        Kb = [A.alloc([8192], BF16) for _ in range(2)]
        Ko = [A.alloc([2048], BF16) for _ in range(2)]
        Vb = [A.alloc([64, 65], BF16) for _ in range(2)]
        Vo = [A.alloc([16, 65], BF16) for _ in range(2)]
        Qb = [A.alloc([2048], BF16) for _ in range(2)]
        inres = [Res("in0"), Res("in1")]
        inslot = [S.slot(), S.slot()]
        augres = [Res("aug0"), Res("aug1")]
        augslot = [S.slot(), S.slot()]
        NPT = 5
        PT = [A.alloc([2, TT], BF16) for _ in range(NPT)]
        ptres = [Res("pt") for _ in range(NPT)]
        osb = [A.alloc([TT], F32) for _ in range(2)]
        r_osb = [Res("osb0"), Res("osb1")]
        rden = [A.alloc([TT], F32) for _ in range(2)]
        r_rden = [Res("rden0"), Res("rden1")]
        od = A.alloc([TT], F32)
        od2 = A.alloc([TT], F32)
        r_od, r_od2 = Res("odt"), Res("od2t")
        sq = [A.alloc([TT], BF16) for _ in range(1)]
        sqres = [Res("sqb")]
        rs = [A.alloc([TT], F32) for _ in range(1)]
        rsres = [Res("rsb")]
        ostg = [A.alloc([TT], BF16) for _ in range(2)]
        ostgres = [Res("ostg0"), Res("ostg1")]
        ostgslot = [S.slot(), S.slot()]
        cnt = {"pt": 0, "o": 0, "bp": 0}
        hc = 0
        for kind in (0, 1):
            for s_ in range(2):
                pairs = []
                if kind == 0:
                    pairs.append((Kb[s_][96:128, :], self.dv(self.kaug_d, 1 * 32 * 8192, (8192, 32), (1, 8192))))
                    pairs.append((Ko[s_][96:128, :], self.dv(self.kaugo_d, 1 * 32 * 2048, (2048, 32), (1, 2048))))
                    for r in range(4):
                        pairs.append((Kb[s_][64:96, r * 2048:(r + 1) * 2048],
                                      self.gout_ap(r, "mR", 0, 0, (2048, 32), (1, 2048))))
                    pairs.append((Ko[s_][64:96, :], self.gin_ap("mR", 0, 0, (2048, 32), (1, 2048))))
                else:
                    for r0 in (0, 96):
                        pairs.append((Kb[s_][r0:r0 + 32, :], self.dv(self.kaug_d, 0, (8192, 32), (1, 8192))))
                        pairs.append((Ko[s_][r0:r0 + 32, :], self.dv(self.kaugo_d, 0, (2048, 32), (1, 2048))))
                self.DMA("sp", augslot[s_], pairs, writes=[augres[s_], inres[s_], self.r_kaug,
                                                            self.rg_in[self.ci("mR")], self.rg_out[self.ci("mR")]])
            def head_loads(hh, s_):
                ksec = "mK" if kind == 0 else "dK"
                vsec = "mV" if kind == 0 else "dV"
                qd = self.q_mla_d if kind == 0 else self.q_diff_d
                pairs = []
                kr0 = 0 if kind == 0 else 32
                for r in range(4):
                    pairs.append((Kb[s_][kr0:kr0 + 64, r * 2048:(r + 1) * 2048],
                                  self.gout_ap(r, ksec, hh, 0, (2048, 64), (1, 2048))))
                    pairs.append((Vb[s_][:, r * 16:(r + 1) * 16, :].rearrange("p b c -> p (b c)"),
                                  self.gout_ap(r, vsec, hh, 0, (1040, 128), (1, 1040))))
                pairs.append((Ko[s_][kr0:kr0 + 64, :], self.gin_ap(ksec, hh, 0, (2048, 64), (1, 2048))))
                pairs.append((Vo[s_].rearrange("p b c -> p (b c)"), self.gin_ap(vsec, hh, 0, (1040, 128), (1, 1040))))
                pairs.append((Qb[s_], self.dv(qd, hh * 128 * 2048, (2048, 128), (1, 2048))))
                ck, cv = self.ci(ksec, hh), self.ci(vsec, hh)
                self.DMA("sp", inslot[s_], pairs,
                         writes=[inres[s_], self.rg_in[ck], self.rg_out[ck], self.rg_in[cv], self.rg_out[cv],
                                 self.r_qm if kind == 0 else self.r_qd])

            nmap = 1 if kind == 0 else 2
            tiles = []
            for hh in range(8):
                s_ = (hc + hh) % 2
                K, KO, V, VO, Q = Kb[s_], Ko[s_], Vb[s_], Vo[s_], Qb[s_]
                for j in range(NT):
                    steps = []
                    for j2 in range(j + 1):
                        for r in range(4):
                            if j2 == j and r == (3 if j % 2 == 0 else 0):
                                continue
                            for m in range(4):
                                c0 = r * 2048 + j2 * 512 + m * 128
                                steps.append((K[:, c0:c0 + 128], V[:, r * 16 + j2 * 4 + m, :], None))
                    for m in range(4):
                        c0 = j * 512 + m * 128
                        steps.append((KO[:, c0:c0 + 128], VO[:, j * 4 + m, :], m))
                    n = len(steps)
                    if kind == 0:
                        units = [((2 * u_, 0), (2 * u_ + 1, 0)) for u_ in range(n // 2)]
                    else:
                        units = [((u_, 0), (u_, 1)) for u_ in range(n)]
                    tiles.append(dict(hh=hh, j=j, s=s_, steps=steps, units=units, Q=Q,
                                      rin=[inres[s_], augres[s_]]))
            glist = [(ti, ui) for ti, tl_ in enumerate(tiles) for ui in range(len(tl_["units"]))]
            ubank = {}

            def smm(g):
                ti, ui = glist[g]
                tl_ = tiles[ti]
                qsl = slice(tl_["j"] * TT, (tl_["j"] + 1) * TT)
                bp = freep.pop(0)
                ubank[g] = bp
                for half, (si_, mp) in enumerate(tl_["units"][ui]):
                    kk, vv, dm = tl_["steps"][si_]
                    rows = slice(0, 128) if kind == 0 else slice(64 * mp, 64 * mp + 64)
                    pr = [(kk[rows], tl_["Q"][rows, qsl])]
                    rd = list(tl_["rin"])
                    if dm is not None:
                        pr.append((self.ident[:, :], self.tri[:, dm, :]))
                        rd.append(self.r_tri)
                    self.MM(ps[:, bp + half, :], pr, rd, [self.pres[bp + half]])

            def pv(g):
                ti, ui = glist[g]
                tl_ = tiles[ti]
                n = len(tl_["steps"])
                bp = ubank.pop(g)
                pi_ = cnt["pt"] % NPT
                cnt["pt"] += 1
                self.ACT(PT[pi_], ps[:, bp:bp + 2, :], AF.Exp, [self.pres[bp], self.pres[bp + 1]], [ptres[pi_]])
                freep.append(bp)
                for half, (si_, mp) in enumerate(tl_["units"][ui]):
                    kk, vv, dm = tl_["steps"][si_]
                    ob = obank(ti, mp)
                    self.MM(ps[0:65, ob, :], [(vv, PT[pi_][:, half, :])], [ptres[pi_]] + tl_["rin"], [self.pres[ob]],
                            start=(si_ == 0), stop=(si_ == n - 1))

            pending = []

            def obank(ti, mp):
                return 6 + (ti % 2) if kind == 0 else 6 + mp

            def finalize(ti, g):
                tl_ = tiles[ti]
                hh, j = tl_["hh"], tl_["j"]
                oi = cnt["o"] % 2
                cnt["o"] += 1
                P64 = slice(0, 64)
                for mp in range(nmap):
                    ob = obank(ti, mp)
                    if mp == 0:
                        self.CP("dve", osb[mp][0:65], ps[0:65, ob, :], [self.pres[ob]], [r_osb[mp]])
                    else:
                        self.ACT(osb[mp][0:65], ps[0:65, ob, :], AF.Copy, [self.pres[ob]], [r_osb[mp]])
                for mp in range(nmap):
                    self.RECIP(rden[mp][64:65], osb[mp][64:65], [r_osb[mp]], [r_rden[mp]])
                if kind == 1:
                    self.TS("dve", rden[1][64:65], rden[1][64:65], self.tab2[64:65, l:l + 1], None, ALU.mult, None,
                            [r_rden[1], self.r_tab2], [r_rden[1]])

                def store():
                    self.DMA("pool", ostgslot[oi], [(self.dv(self.o_d, kind * 512 * 2048 + hh * 64 * 2048 + j * TT,
                                                             (2048, 64), (1, TT)), ostg[oi][0:64])],
                             reads=[ostgres[oi], self.r_od])

                def stage2():
                    bp = freep.pop(0)
                    for mp in range(nmap):
                        self.MM(ps[0:64, bp + mp, :], [(self.ones[64:65, 0:64], rden[mp][64:65])],
                                [r_rden[mp], self.onesres], [self.pres[bp + mp]])
                    if kind == 0:
                        self.TT("dve", ostg[oi][P64], osb[0][P64], ps[P64, bp, :], ALU.mult,
                                [r_osb[0], self.pres[bp]], [ostgres[oi]])
                        store()
                    else:
                        self.TT("dve", od[P64], osb[0][P64], ps[P64, bp, :], ALU.mult, [r_osb[0], self.pres[bp]], [r_od])
                        self.TT("dve", od2[P64], osb[1][P64], ps[P64, bp + 1, :], ALU.mult,
                                [r_osb[1], self.pres[bp + 1]], [r_od2])
                        self.TT("dve", od[P64], od[P64], od2[P64], ALU.add, [r_od, r_od2], [r_od])
                        self.TT("dve", sq[0][P64], od[P64], od[P64], ALU.mult, [r_od], [sqres[0]])
                    freep.append(bp)

                def stage3():
                    bp = freep.pop(0)
                    self.MM(ps[P64, bp, :], [(self.ones_bf[P64, 0:64], sq[0][P64])], [sqres[0], self.onesres],
                            [self.pres[bp]])
                    self.ACT(rs[0][P64], ps[P64, bp, :], AF.Ln, [self.pres[bp]], [rsres[0]], scale=1.0 / 64, bias=EPS)
                    freep.append(bp)
                    self.ACT(rs[0][P64], rs[0][P64], AF.Exp, [rsres[0]], [rsres[0]], scale=-0.5)
                    self.STT("dve", ostg[oi][P64], od[P64], self.tab2[P64, 2 + l:3 + l], rs[0][P64], ALU.mult, ALU.mult,
                             [r_od, rsres[0], self.r_tab2], [ostgres[oi]])
                    store()
                pending.append((g + 5, stage2))
                if kind == 1:
                    pending.append((g + 9, stage3))

            def run_pending(g, flush=False):
                keep = []
                for (due, fn) in pending:
                    if flush or due <= g:
                        fn()
                    else:
                        keep.append((due, fn))
                pending[:] = keep

            freep = [0, 2, 4]
            LOOK = 2
            ng = len(glist)
            head_loads(0, hc % 2)
            for g in range(min(LOOK, ng)):
                smm(g)
            for g in range(ng):
                ti, ui = glist[g]
                if ui == 0 and tiles[ti]["j"] == 0 and tiles[ti]["hh"] + 1 < 8:
                    head_loads(tiles[ti]["hh"] + 1, (hc + tiles[ti]["hh"] + 1) % 2)
                if g + LOOK < ng:
                    smm(g + LOOK)
                pv(g)
                run_pending(g)
                if ui == len(tiles[ti]["units"]) - 1:
                    finalize(ti, g)
            run_pending(ng, flush=True)
            hc += 8
        A.pop()

    def phaseC0(self, l):
        A, S = self.A, self.S
        ps = self.psum
        A.push()
        ptok = A.alloc([16, 512], BF16)
        tails = A.alloc([2, 512], BF16)
        band = A.alloc([3, 4, 128], BF16)
        tsel = A.alloc([2, 4, 4, 128], BF16)
        pw = A.alloc([4, 128], BF16)
        r_in = Res("c0in")
        r_w = Res("c0w")
        sl1, sl2 = S.slot(), S.slot()
        pairs = [(ptok, self.dv(self.p_d, 0, (512, 128), (128 * 512, 16), (1, 512)))]
        for r in range(4):
            pairs.append((tails[(r % 2) * 64:(r % 2) * 64 + 64, r // 2, :],
                          self.gout_ap(r, "tl", 0, 0, (512, 64), (1, 512))))
        self.DMA("sp", sl1, pairs, writes=[r_in, self.r_pd, self.rg_out[self.ci("tl")]])
        pwv = self.w["pool_w"].ap()[l].rearrange("g c d -> c g d")
        self.DMA("pool", sl2, [(band, self.c_band[:, :, :, :]), (tsel, self.c_tailsel[:, :, :, :, :]), (pw, pwv)],
                 writes=[r_w])
        pooled = [A.alloc([TT], BF16) for _ in range(2)]
        r_pooled = [Res("pooled0"), Res("pooled1")]
        ystg = [A.alloc([4, TT], BF16) for _ in range(2)]
        r_ystg = [Res("ystg0"), Res("ystg1")]
        yslot = [S.slot(), S.slot()]
        c = 0
        for t in range(NT):
            yi = t % 2
            for g in range(4):
                gsl = slice(g * 128, (g + 1) * 128)
                bk = self.bank()
                for blk in range(4):
                    bb = t * 4 + blk
                    bm = band[:, 2, g, :] if bb == 0 else band[:, 0, g, :]
                    pr = [(ptok[:, bb, gsl], bm)]
                    if blk == 0:
                        pr.append((tails[:, 0, gsl], tsel[:, 0, t, g, :]))
                        pr.append((tails[:, 1, gsl], tsel[:, 1, t, g, :]))
                    else:
                        pr.append((ptok[:, bb - 1, gsl], band[:, 1, g, :]))
                    self.MM(ps[:, bk, blk * 128:(blk + 1) * 128], pr, [r_in, r_w], [self.pres[bk]])
                pi_ = c % 2
                c += 1
                self.ACT(pooled[pi_], ps[:, bk, :], AF.Copy, [self.pres[bk]], [r_pooled[pi_]])
                b2 = self.bank()
                self.MM(ps[:, b2, :], [(pw[:, g, :], pooled[pi_])], [r_pooled[pi_], r_w], [self.pres[b2]])
                self.TS("dve", ystg[yi][:, g, :], ps[:, b2, :], self.ptab[:, l * 4 + g:l * 4 + g + 1],
                        self.ptab[:, 8 + l * 4 + g:8 + l * 4 + g + 1], ALU.add, ALU.mult, [self.pres[b2]], [r_ystg[yi]])
            self.DMA("sp", yslot[yi], [(self.dv(self.o_d, 2 * 512 * 2048 + t * TT, (2048, 128), (128 * 2048, 4), (1, TT)),
                                        ystg[yi])], reads=[r_ystg[yi], self.r_od])
        A.pop()

    def phaseC1(self, l):
        A, S = self.A, self.S
        ps = self.psum
        A.push()
        HT = 1024
        u = A.alloc([KD, HT], BF16)
        ures = [[Res("uc") for t in range(2)] for k in range(KD)]
        y = A.alloc([3, 4, HT], BF16)
        r_y = Res("y")
        yslot = S.slot()
        merged = A.alloc([KD, HT], BF16)
        r_m = [[Res("m") for t in range(2)] for k in range(KD)]
        sq = [A.alloc([TT], BF16) for _ in range(2)]
        sqres = [Res("sq") for _ in range(2)]
        rs = [A.alloc([TT], F32) for _ in range(2)]
        rsres = [Res("rs") for _ in range(2)]
        NW = 2
        gw = [A.alloc([KD, 3, 128], BF16) for _ in range(NW)]
        bw = [A.alloc([4, 3, 128], BF16) for _ in range(NW)]
        r_w = [Res("cw") for _ in range(NW)]
        wslot = [S.slot() for _ in range(NW)]
        ow = [A.alloc([KD, 128], BF16) for _ in range(NW)]
        r_ow = [Res("ow") for _ in range(NW)]
        owslot = [S.slot() for _ in range(NW)]
        sig = [A.alloc([TT], F32) for _ in range(3)]
        r_sig = [Res("sig") for _ in range(3)]
        tA = A.alloc([TT], F32)
        tB = A.alloc([TT], F32)
        r_tA, r_tB = Res("tA"), Res("tB")
        win = self.w["w_in"].ap()[l].rearrange("(k p) c -> p k c", p=128)
        wbr = self.w["w_branch"].ap()[l].rearrange("i (k p) d -> p k i d", p=128)
        wout = self.w["w_out"].ap()[l].rearrange("(k p) d -> p k d", p=128)
        wc = 0
        oc = 0
        for half in range(2):
            tiles = [2 * half, 2 * half + 1]
            for ti, t in enumerate(tiles):
                tsl = slice(t * TT, (t + 1) * TT)
                lsl = slice(ti * TT, (ti + 1) * TT)
                self.rmsnorm_tile([self.h[:, k, tsl] for k in range(KD)], [self.hres[k][t] for k in range(KD)],
                                  [u[:, k, lsl] for k in range(KD)], [ures[k][ti] for k in range(KD)],
                                  [self.gains[:, 16 + l * 8 + k:17 + l * 8 + k] for k in range(KD)], D,
                                  sq, sqres, rs, rsres, stat_bank=7)
            self.DMA("sp", yslot, [(y[:, br, :, :], self.dv(self.o_d, br * 512 * 2048 + half * HT, (2048, 128),
                                                           (128 * 2048, 4), (1, HT))) for br in range(3)],
                     writes=[r_y, self.r_od])
            for d in range(KD):
                wi = wc % NW
                wc += 1
                dsl = slice(d * 128, (d + 1) * 128)
                pairs = [(gw[wi][:, :, i, :], win[:, :, 2464 + i * 1024 + d * 128:2464 + i * 1024 + (d + 1) * 128])
                         for i in range(3)]
                pairs += [(bw[wi][:, :, i, :], wbr[:, :, i, dsl]) for i in range(3)]
                self.DMA("pool", wslot[wi], pairs, writes=[r_w[wi]])
                for ti, t in enumerate(tiles):
                    lsl = slice(ti * TT, (ti + 1) * TT)
                    gb = []
                    for i in range(3):
                        bk = self.bank()
                        gb.append(bk)
                        self.MM(ps[:, bk, :], [(gw[wi][:, k, i, :], u[:, k, lsl]) for k in range(KD)],
                                [ures[k][ti] for k in range(KD)] + [r_w[wi]], [self.pres[bk]])
                        self.ACT(sig[i], ps[:, bk, :], AF.Sigmoid, [self.pres[bk]], [r_sig[i]])
                    bb = []
                    for i in range(3):
                        bk = self.bank()
                        bb.append(bk)
                        self.MM(ps[:, bk, :], [(bw[wi][:, k, i, :], y[:, i, k, lsl]) for k in range(4)],
                                [r_y, r_w[wi]], [self.pres[bk]])
                    self.TT("dve", tA, sig[0], ps[:, bb[0], :], ALU.mult, [r_sig[0], self.pres[bb[0]]], [r_tA])
                    self.TT("dve", tB, sig[1], ps[:, bb[1], :], ALU.mult, [r_sig[1], self.pres[bb[1]]], [r_tB])
                    self.TT("dve", tA, tA, tB, ALU.add, [r_tA, r_tB], [r_tA])
                    self.TT("dve", tB, sig[2], ps[:, bb[2], :], ALU.mult, [r_sig[2], self.pres[bb[2]]], [r_tB])
                    self.TT("dve", merged[:, d, lsl], tA, tB, ALU.add, [r_tA, r_tB], [r_m[d][ti]])
            for d2 in range(KD):
                oi = oc % NW
                oc += 1
                self.DMA("pool", owslot[oi], [(ow[oi], wout[:, :, d2 * 128:(d2 + 1) * 128])], writes=[r_ow[oi]])
                for ti, t in enumerate(tiles):
                    tsl = slice(t * TT, (t + 1) * TT)
                    lsl = slice(ti * TT, (ti + 1) * TT)
                    bk = self.bank()
                    self.MM(ps[:, bk, :], [(ow[oi][:, k, :], merged[:, k, lsl]) for k in range(KD)],
                            [r_m[k][ti] for k in range(KD)] + [r_ow[oi]], [self.pres[bk]])
                    self.TT("dve", self.h[:, d2, tsl], self.h[:, d2, tsl], ps[:, bk, :], ALU.add,
                            [self.pres[bk], self.hres[d2][t]], [self.hres[d2][t]])
        A.pop()


def chunk_of(c, j):
    return (4 * j + c) if (j % 2 == 0) else (4 * j + 3 - c)


def token_index(c):
    idx = []
    for j in range(4):
        g = chunk_of(c, j)
        idx.append(np.arange(g * 512, (g + 1) * 512))
    return np.concatenate(idx)


def fm(v):
    v = np.asarray(v, np.float32)
    return np.ascontiguousarray(v.reshape(-1, 128).T)


_CACHE = {}
_LAST = {}


def _host_constants():
    tri = np.zeros((128, 4, 512), np.float32)
    k = np.arange(128)[:, None]
    q = np.arange(512)[None, :]
    for m in range(4):
        tri[:, m, :] = np.where(128 * m + k <= q, 0.0, -30000.0)
    band = np.zeros((4, 128, 4, 128), np.float32)
    tp = np.arange(128)[:, None]
    t = np.arange(128)[None, :]
    bands = np.zeros((3, 4, 128, 128), np.float32)
    for g, w in enumerate((2, 4, 8, 16)):
        inwin = ((t - tp) >= 0) & ((t - tp) <= w - 1)
        bands[0, g] = inwin / w - (t == tp)
        bands[1, g] = (tp >= t + 129 - w) / w
        cntv = np.minimum(t + 1, w).astype(np.float32)
        bands[2, g] = inwin / cntv - (t == tp)
    return tri, bands


def kernel(**inputs):
    stage = inputs.pop("_stage", 99)
    x = np.asarray(inputs["x"], np.float32)
    pos = np.asarray(inputs["positions"]).astype(np.int32)
    if stage not in _CACHE:
        _CACHE[stage] = Prog(stage)
    prog = _CACHE[stage]
    f32 = lambda a: np.ascontiguousarray(np.asarray(a, np.float32))

    gains = np.zeros((128, 64), np.float32)
    ptab = np.zeros((128, 32), np.float32)
    lam = np.zeros((128, 8, 32), np.float32)
    for l in range(DEPTH):
        gains[:, 0 + l * 8:8 + l * 8] = fm(inputs["ffn1_norm"][l])
        gains[:, 16 + l * 8:24 + l * 8] = fm(inputs["mix_norm"][l])
        gains[:, 32 + l * 8:40 + l * 8] = fm(inputs["ffn2_norm"][l])
        gains[:, 56 + 2 * l:58 + 2 * l] = fm(inputs["mla_q_norm"][l])
        gains[:, 60 + l] = np.asarray(inputs["mla_kv_norm"][l], np.float32)
        gains[:, 62 + l] = np.tile(np.asarray(inputs["diff_subln"][l], np.float32), 2)
        ptab[:, l * 4:l * 4 + 4] = np.asarray(inputs["pool_b"][l], np.float32).T
        ptab[:, 8 + l * 4:12 + l * 4] = fm(inputs["pool_scale"][l])
        for i, nm in enumerate(("diff_lambda_q1", "diff_lambda_k1", "diff_lambda_q2", "diff_lambda_k2")):
            lam[:, l * 4 + i, :] = np.asarray(inputs[nm][l], np.float32)[None, :]
    gains[:, 48:56] = fm(inputs["final_norm"])
    inv_freq = (np.float32(10000.0) ** (-np.arange(16, dtype=np.float32) / np.float32(16))).astype(np.float32)
    ptab[64:96, 16] = np.tile(inv_freq, 2)
    ptab[64:80, 17] = -1.0
    ptab[80:96, 17] = 1.0
    tri, bands = _host_constants()
    common = {
        "gains": gains, "ptab": ptab, "lam": lam,
        "ones": np.ones((128, 128), np.float32),
        "tri": tri, "ident": np.eye(128, dtype=np.float32),
    }
    for nm in ("ffn1_w_gate", "ffn1_w_up", "ffn1_w_down", "ffn2_w_gate", "ffn2_w_up", "ffn2_w_down",
               "w_in", "mla_w_uq", "mla_w_ukv", "pool_w", "w_branch", "w_out"):
        if nm in prog.din:
            common[nm] = f32(inputs[nm])
    in_maps = []
    for core in range(NCORES):
        b, c = divmod(core, 4)
        idx = token_index(c)
        m = dict(common)
        m["xT"] = np.ascontiguousarray(x[b, idx, :].T)
        if "pos_own" in prog.din:
            po = pos[b, idx]
            m["pos_own"] = np.ascontiguousarray(np.broadcast_to(po[None, :], (32, T)))
            m["pos_own16"] = np.ascontiguousarray(po.reshape(128, 16))
            gidx = np.concatenate([token_index(r) for r in range(4)])
            m["pos_g64"] = np.ascontiguousarray(pos[b, gidx].reshape(128, 64))
            fl = np.zeros((4, 8192), np.float32)
            for r in range(4):
                for j2 in range(4):
                    if chunk_of(r, j2) >= chunk_of(c, j2):
                        fl[j2, r * 2048 + j2 * 512:r * 2048 + (j2 + 1) * 512] = 1.0
            m["flags"] = np.ascontiguousarray(fl.reshape(4, 128, 64))
            bd = np.zeros((128, 3, 4, 128), np.float32)
            for g in range(4):
                bd[:, 0, g, :] = bands[0, g]
                bd[:, 1, g, :] = bands[1, g]
                bd[:, 2, g, :] = bands[2, g] if c == 0 else bands[0, g]
            m["band"] = bd
            ts = np.zeros((128, 2, 4, 4, 128), np.float32)
            for j in range(4):
                G = chunk_of(c, j)
                if G == 0:
                    continue
                found = [(r, j2) for r in range(4) for j2 in range(4) if chunk_of(r, j2) == G - 1]
                r, j2 = found[0]
                for g, w in enumerate((2, 4, 8, 16)):
                    for tok in range(16):
                        for tt in range(w - 1):
                            if tok >= tt - w + 17:
                                ts[(r % 2) * 64 + j2 * 16 + tok, r // 2, j, g, tt] = 1.0 / w
            m["tailsel"] = ts
        in_maps.append({k: v for k, v in m.items() if k in prog.din})
    res = run_bass_kernel_spmd(prog.nc, in_maps, core_ids=list(range(NCORES)))
    _LAST["res"] = res.results
    out = np.empty((B, S, D), np.float32)
    for core in range(NCORES):
        b, c = divmod(core, 4)
        out[b, token_index(c), :] = np.asarray(res.results[core]["outT"]).T
    return out
```

```python
import math
import numpy as np
import concourse.bass as bass
import concourse.mybir as mybir
from concourse.bass_utils import run_bass_kernel_spmd

F32 = mybir.dt.float32
BF16 = mybir.dt.bfloat16
I32 = mybir.dt.int32
AF = mybir.ActivationFunctionType
ALU = mybir.AluOpType

NCORES = 8
D = 1024
DFF = 2816
DEPTH = 2
B = 2
S = 8192
T = 2048
TT = 512
NT = T // TT
KD = D // 128
EPS = 1e-6
INW = 5536
SC_MLA = 96.0 ** -0.5
SC_DIFF = 32.0 ** -0.5


class Res:
    __slots__ = ("name", "w", "r")

    def __init__(self, name):
        self.name = name
        self.w = None
        self.r = {}


class Slot:
    def __init__(self):
        self.sem = None
        self.count = 0
        self.kind = None


class Sched:
    ENG = ("pe", "act", "dve", "pool", "sp")

    def __init__(self, nc):
        self.nc = nc
        self.prog = {e: [] for e in self.ENG}
        self.esem = {e: nc.alloc_semaphore("S_" + e) for e in ("pe", "act", "dve", "pool")}
        self.ecnt = {e: 0 for e in self.esem}
        self.waited = {e: {} for e in self.ENG}
        self.nslots = 0
        self.slots = []
        self.customs = []
        self.free_sems = {"sp": [], "pool": []}
        self.semcount = {}
        self.scopes = []

    def slot(self, name=None):
        sl = Slot()
        if self.scopes:
            self.scopes[-1].append(sl)
        return sl

    def _bind(self, sl, q):
        if sl.sem is None:
            pool = self.free_sems[q]
            if pool:
                sl.sem, sl.count = pool.pop()
            else:
                self.nslots += 1
                sl.sem, sl.count = self.nc.alloc_semaphore("D%d" % self.nslots), 0
            sl.kind = q
        assert sl.kind == q, "a DMA semaphore must stay on one queue type"

    def scope_push(self):
        self.scopes.append([])

    def scope_pop(self):
        self.barrier()
        for sl in self.scopes.pop():
            if sl.sem is not None:
                self.free_sems[sl.kind].append((sl.sem, sl.count))
                sl.sem = None

    def _deps(self, eng, reads, writes):
        deps = {}

        def add(d):
            if d is None:
                return
            sem, val, deng = d
            k = id(sem)
            if k not in deps or deps[k][1] < val:
                deps[k] = (sem, val, deng)

        for r in reads:
            add(r.w)
        for w in writes:
            add(w.w)
            for d in w.r.values():
                add(d)
        for k, (sem, val, deng) in deps.items():
            if deng == eng and eng == "pe":
                continue
            if self.waited[eng].get(k, 0) >= val:
                continue
            self.waited[eng][k] = val
            self.prog[eng].append(("wait", sem, val))

    def _mark(self, tok, reads, writes):
        k = id(tok[0])
        for r in reads:
            r.r[k] = tok
        for w in writes:
            w.w = tok
            w.r = {}

    def op(self, eng, fn, reads=(), writes=()):
        self._deps(eng, reads, writes)
        self.ecnt[eng] += 1
        tok = (self.esem[eng], self.ecnt[eng], eng)
        self.prog[eng].append(("op", fn, self.esem[eng], 1))
        self._mark(tok, reads, writes)

    def dma(self, q, slot, fns, reads=(), writes=()):
        self._bind(slot, q)
        self._deps(q, reads, writes)
        for fn in fns:
            slot.count += 16
            self.prog[q].append(("op", fn, slot.sem, 16))
        self.semcount[id(slot.sem)] = (slot.sem, slot.count)
        tok = (slot.sem, slot.count, "dma")
        self._mark(tok, reads, writes)

    def custom(self, q, fn, sem, val, reads=(), writes=()):
        self._deps(q, reads, writes)
        self.prog[q].append(("opc", fn, sem))
        self.customs.append((sem, val))
        tok = (sem, val, "cc")
        self._mark(tok, reads, writes)

    def barrier(self):
        targets = []
        for e, sem in self.esem.items():
            if self.ecnt[e] > 0:
                targets.append((sem, self.ecnt[e]))
        for (sem, cnt_) in self.semcount.values():
            targets.append((sem, cnt_))
        for eng in self.ENG:
            for sem, val in targets:
                k = id(sem)
                if self.waited[eng].get(k, 0) >= val:
                    continue
                self.waited[eng][k] = val
                self.prog[eng].append(("wait", sem, val))

    def final_wait(self, eng, res_list):
        self._deps(eng, res_list, ())

    def emit(self, eng_name, e):
        for it in self.prog[eng_name]:
            if it[0] == "wait":
                e.wait_ge(it[1], it[2])
            elif it[0] == "op":
                ins = it[1](e)
                ins.then_inc(it[2], it[3])
            else:
                ins = it[1](e)
                ins.then_inc(it[2])


class Arena:
    def __init__(self, nc, nbytes):
        self.t = nc.alloc_sbuf_tensor("arena", [128, nbytes // 4], F32)
        self.nbytes = nbytes
        self.off = 0
        self.marks = []

    def alloc(self, free_shape, dtype):
        esz = 4 if dtype in (F32, I32) else 2
        n = int(np.prod(free_shape)) * esz
        n_al = (n + 63) // 64 * 64
        assert self.off + n_al <= self.nbytes, ("SBUF arena overflow", self.off, n_al, self.nbytes)
        a = self.t[:, self.off // 4:(self.off + n_al) // 4]
        self.off += n_al
        if dtype != F32:
            a = a.bitcast(dtype)
        a = a[:, 0:int(np.prod(free_shape))]
        if len(free_shape) == 2:
            a = a.rearrange("p (a b) -> p a b", a=free_shape[0])
        elif len(free_shape) == 3:
            a = a.rearrange("p (a b c) -> p a b c", a=free_shape[0], b=free_shape[1])
        elif len(free_shape) == 4:
            a = a.rearrange("p (a b c d) -> p a b c d", a=free_shape[0], b=free_shape[1], c=free_shape[2])
        return a

    def push(self):
        self.marks.append(self.off)
        if self.on_push is not None:
            self.on_push()

    on_push = None

    def pop(self):
        self.off = self.marks.pop()
        if self.on_pop is not None:
            self.on_pop()

    on_pop = None


class Prog:
    def __init__(self, stage=99):
        self.stage = stage
        nc = bass.Bass("TRN2", target_bir_lowering=False)
        self.nc = nc
        self.S = Sched(nc)
        self.A = Arena(nc, 200 * 1024)
        self.A.on_pop = self.S.scope_pop
        self.A.on_push = self.S.scope_push
        self.din = {}
        self.build()

    def inp(self, name, shape, dtype=F32):
        t = self.nc.dram_tensor(name, list(shape), dtype, kind="ExternalInput")
        self.din[name] = t
        return t

    def build(self):
        nc, S, A = self.nc, self.S, self.A
        st = self.stage
        xT = self.inp("xT", [D, T])
        gains = self.inp("gains", [128, 64])
        ptab = self.inp("ptab", [128, 32])
        ones_in = self.inp("ones", [128, 128])
        w = {}
        self.noffn = st in (1.5, 2.5, 2.7, 3, 4)
        if not self.noffn:
            for nm in ("ffn1_w_gate", "ffn1_w_up", "ffn2_w_gate", "ffn2_w_up"):
                w[nm] = self.inp(nm, [DEPTH, D, DFF])
            for nm in ("ffn1_w_down", "ffn2_w_down"):
                w[nm] = self.inp(nm, [DEPTH, DFF, D])
        if st >= 1.5:
            w["w_in"] = self.inp("w_in", [DEPTH, D, INW]) if st >= 2 else None
            if st >= 2:
                w["mla_w_uq"] = self.inp("mla_w_uq", [DEPTH, 256, 768])
                w["mla_w_ukv"] = self.inp("mla_w_ukv", [DEPTH, 128, 1024])
                w["pool_w"] = self.inp("pool_w", [DEPTH, 4, 128, 128])
                w["w_branch"] = self.inp("w_branch", [DEPTH, 3, 512, D])
                w["w_out"] = self.inp("w_out", [DEPTH, D, D])
            self.c_lam = self.inp("lam", [128, 8, 32])
            self.c_pos_own = self.inp("pos_own", [32, T], I32)
            self.c_pos_own16 = self.inp("pos_own16", [128, 16], I32)
            self.c_pos_g64 = self.inp("pos_g64", [128, 64], I32)
            self.c_flags = self.inp("flags", [4, 128, 64])
            self.c_tri = self.inp("tri", [128, 4, 512])
            self.c_ident = self.inp("ident", [128, 128])
            self.c_band = self.inp("band", [128, 3, 4, 128])
            self.c_tailsel = self.inp("tailsel", [128, 2, 4, 4, 128])
        self.w = w
        outT = nc.dram_tensor("outT", [D, T], F32, kind="ExternalOutput")
        self.dbg = {}

        self.h = A.alloc([KD, T], F32)
        self.hres = [[Res("h%d_%d" % (k, t)) for t in range(NT)] for k in range(KD)]
        self.gains = A.alloc([64], F32)
        self.gres = Res("gains")
        self.ptab = A.alloc([32], F32)
        self.ones = A.alloc([128], F32)
        self.ones_bf = A.alloc([128], BF16)
        self.onesres = Res("ones")
        self.psum = nc.alloc_psum_tensor("psum", [128, 8, 512], F32)
        self.pres = [Res("ps%d" % i) for i in range(8)]
        self.ld = S.slot("ld0")

        self.DMA("sp", self.ld, [(self.gains, gains[:, :]), (self.ones, ones_in[:, :]), (self.ptab, ptab[:, :])],
                 writes=[self.gres, self.onesres])
        self.CP("dve", self.ones_bf, self.ones, [self.onesres], [self.onesres])
        xv = xT.ap().rearrange("(k p) t -> p k t", p=128)
        for k in range(KD):
            self.DMA("sp", S.slot("ldx%d" % k), [(self.h[:, k, :], xv[:, k, :])], writes=self.hres[k])

        if st >= 1.5:
            self.setup_mixer()
            self.setup_mixer_compute()
            self._setup_done = True
        if st == 1.5:
            tabdbg = nc.dram_tensor("tabdbg", [2, 32, T], F32, kind="ExternalOutput")
            r = Res("tabdbg")
            self.DMA("sp", S.slot("tdbg"), [(tabdbg.ap()[0], self.cos_t[64:96]), (tabdbg.ap()[1], self.sin_t[64:96])],
                     reads=[self.r_rope], writes=[r])
            self.dbg_res = [r]

        for l in range(DEPTH):
            if st <= 0:
                break
            if not self.noffn:
                self.ffn(l, w["ffn1_w_gate"], w["ffn1_w_up"], w["ffn1_w_down"], gcol=0 + l * 8)
            if st <= 1.5:
                break
            if not self._setup_done:
                self.setup_mixer_compute()
                self._setup_done = True
            self.mixer(l)
            if st < 5:
                break
            self.ffn(l, w["ffn2_w_gate"], w["ffn2_w_up"], w["ffn2_w_down"], gcol=32 + l * 8)
            if st < 6:
                break
        if st >= 7:
            self.final_norm()

        ov = outT.ap().rearrange("(k p) t -> p k t", p=128)
        self.st = S.slot("st")
        allh = [r for row in self.hres for r in row]
        for k in range(KD):
            self.DMA("sp", self.st, [(ov[:, k, :], self.h[:, k, :])], reads=self.hres[k])
        S.final_wait("sp", allh + list(self.dbg_res))

        with nc.Block() as block:
            @block.tensor
            def _(e):
                S.emit("pe", e)

            @block.scalar
            def _(e):
                S.emit("act", e)

            @block.vector
            def _(e):
                S.emit("dve", e)

            @block.gpsimd
            def _(e):
                S.emit("pool", e)

            @block.sync
            def _(e):
                S.emit("sp", e)

    dbg_res = ()

    def final_norm(self):
        A = self.A
        A.push()
        sq = [A.alloc([TT], BF16) for _ in range(2)]
        sqres = [Res("fsq%d" % i) for i in range(2)]
        rs = [A.alloc([TT], F32) for _ in range(2)]
        rsres = [Res("frs%d" % i) for i in range(2)]
        self.rmsnorm_fm(self.h, self.hres, self.h, self.hres, 48, KD, D, sq, sqres, rs, rsres, stat_bank=7)
        A.pop()

    def ACT(self, out, in_, func, reads, writes, **kw):
        self.S.op("act", lambda e: e.activation(out=out, in_=in_, func=func, **kw), reads, writes)

    def MM(self, out, pairs, reads, writes, start=True, stop=True):
        n = len(pairs)

        def fn(e):
            ins = None
            for i, (l, r) in enumerate(pairs):
                ins = e.matmul(out, l, r, start=(start and i == 0), stop=(stop and i == n - 1))
            return ins
        self.S.op("pe", fn, reads, writes)

    def MMI(self, outs, pair_lists, reads, writes):
        n = len(pair_lists[0])

        def fn(e):
            ins = None
            for i in range(n):
                for o, pl in zip(outs, pair_lists):
                    l_, r_ = pl[i]
                    ins = e.matmul(o, l_, r_, start=(i == 0), stop=(i == n - 1))
            return ins
        self.S.op("pe", fn, reads, writes)

    def STT(self, eng, out, in0, scalar, in1, op0, op1, reads, writes):
        self.S.op(eng, lambda e: e.scalar_tensor_tensor(out, in0, scalar, in1, op0, op1), reads, writes)

    def TT(self, eng, out, in0, in1, op, reads, writes):
        self.S.op(eng, lambda e: e.tensor_tensor(out, in0, in1, op), reads, writes)

    def TS(self, eng, out, in0, s1, s2, op0, op1, reads, writes):
        if op1 is None:
            self.S.op(eng, lambda e: e.tensor_scalar(out, in0, s1, None, op0), reads, writes)
        else:
            self.S.op(eng, lambda e: e.tensor_scalar(out, in0, s1, s2, op0, op1), reads, writes)

    def CP(self, eng, out, in_, reads, writes):
        self.S.op(eng, lambda e: e.tensor_copy(out, in_), reads, writes)

    def RECIP(self, out, in_, reads, writes):
        self.S.op("dve", lambda e: e.reciprocal(out, in_), reads, writes)

    def MEMSET(self, eng, out, val, writes):
        self.S.op(eng, lambda e: e.memset(out, val), (), writes)

    def DMA(self, q, slot, pairs, reads=(), writes=()):
        fns = [(lambda e, o=o, i=i: e.dma_start(out=o, in_=i)) for (o, i) in pairs]
        self.S.dma(q, slot, fns, reads, writes)

    def rmsnorm_tile(self, srcs, sres, dsts, dres, gcols, width, sq, sqres, rs, rsres, stat_bank, npart=128,
                     gtab=None):
        ps = self.psum
        P = slice(0, npart)
        nk = len(srcs)
        for k in range(nk):
            b = self._sqi % len(sq)
            self._sqi += 1
            self.ACT(sq[b][P], srcs[k], AF.Square, [sres[k]], [sqres[b]])
            self.MM(ps[P, stat_bank, :], [(self.ones_bf[P, 0:npart], sq[b][P])], [sqres[b], self.onesres],
                    [self.pres[stat_bank]], start=(k == 0), stop=(k == nk - 1))
        r = self._rsi % len(rs)
        self._rsi += 1
        self.ACT(rs[r][P], ps[P, stat_bank, :], AF.Sqrt, [self.pres[stat_bank]], [rsres[r]],
                 scale=1.0 / width, bias=EPS)
        self.RECIP(rs[r][P], rs[r][P], [rsres[r]], [rsres[r]])
        for k in range(nk):
            self.STT("dve", dsts[k], srcs[k], gcols[k], rs[r][P], ALU.mult, ALU.mult,
                     [sres[k], rsres[r], self.gres], [dres[k]])

    _sqi = 0
    _rsi = 0

    def rmsnorm_fm(self, src, sres, dst, dres, gcol, nk, width, sq, sqres, rs, rsres, stat_bank, tiles=None):
        for t in (range(NT) if tiles is None else tiles):
            tsl = slice(t * TT, (t + 1) * TT)
            self.rmsnorm_tile([src[:, k, tsl] for k in range(nk)], [sres[k][t] for k in range(nk)],
                              [dst[:, k, tsl] for k in range(nk)], [dres[k][t] for k in range(nk)],
                              [self.gains[:, gcol + k:gcol + k + 1] for k in range(nk)], width,
                              sq, sqres, rs, rsres, stat_bank)

    def ffn(self, l, wg_d, wu_d, wd_d, gcol):
        S, A = self.S, self.A
        ps = self.psum
        A.push()
        G = 2
        NG = DFF // (128 * G)
        xn = A.alloc([KD, T], BF16)
        xres = [[Res("xn%d_%d" % (k, t)) for t in range(NT)] for k in range(KD)]
        sq = [A.alloc([TT], BF16) for _ in range(4)]
        sqres = [Res("sq%d" % i) for i in range(4)]
        rs = [A.alloc([TT], F32) for _ in range(2)]
        rsres = [Res("rs%d" % i) for i in range(2)]
        NS = 3
        wg = [A.alloc([KD, 128 * G], BF16) for _ in range(NS)]
        wu = [A.alloc([KD, 128 * G], BF16) for _ in range(NS)]
        wd = [A.alloc([G, D], BF16) for _ in range(NS)]
        wres = [Res("w%d" % i) for i in range(NS)]
        wslot = [S.slot() for _ in range(NS)]
        hid = [A.alloc([G, T], BF16) for _ in range(2)]
        hidres = [[[Res("hid") for t in range(NT)] for c in range(G)] for _ in range(2)]
        sil = [A.alloc([TT], F32) for _ in range(2)]
        silres = [Res("sil%d" % i) for i in range(2)]

        self.rmsnorm_fm(self.h, self.hres, xn, xres, gcol, KD, D, sq, sqres, rs, rsres, stat_bank=7)
        if self.stage == 0.5:
            for k in range(KD):
                for t in range(NT):
                    S.op("dve", lambda e, k=k, t=t: e.tensor_copy(self.h[:, k, t * TT:(t + 1) * TT], xn[:, k, t * TT:(t + 1) * TT]),
                         reads=[xres[k][t]], writes=[self.hres[k][t]])
            A.pop()
            return

        wgv = wg_d.ap()[l].rearrange("(k p) c -> p k c", p=128)
        wuv = wu_d.ap()[l].rearrange("(k p) c -> p k c", p=128)
        wdv = wd_d.ap()[l].rearrange("(c p) d -> p c d", p=128)

        def load(gi):
            s = gi % NS
            c0 = gi * 128 * G
            self.DMA("pool", wslot[s], [(wg[s], wgv[:, :, c0:c0 + 128 * G]),
                                        (wu[s], wuv[:, :, c0:c0 + 128 * G]),
                                        (wd[s], wdv[:, gi * G:(gi + 1) * G, :])], writes=[wres[s]])

        cnt = {"gu": 0, "dn": 0, "sil": 0}

        def up(gi):
            s = gi % NS
            hs = gi % 2
            for c in range(G):
                for t in range(NT):
                    tsl = slice(t * TT, (t + 1) * TT)
                    gb = cnt["gu"] % 2
                    ub = 2 + cnt["gu"] % 2
                    cnt["gu"] += 1
                    xr = [xres[k][t] for k in range(KD)]
                    csl = slice(c * 128, (c + 1) * 128)
                    self.MMI([ps[:, gb, :], ps[:, ub, :]],
                             [[(wg[s][:, k, csl], xn[:, k, tsl]) for k in range(KD)],
                              [(wu[s][:, k, csl], xn[:, k, tsl]) for k in range(KD)]],
                             xr + [wres[s]], [self.pres[gb], self.pres[ub]])
                    sb = cnt["sil"] % 2
                    cnt["sil"] += 1
                    self.ACT(sil[sb], ps[:, gb, :], AF.Silu, [self.pres[gb]], [silres[sb]])
                    self.TT("dve", hid[hs][:, c, tsl], ps[:, ub, :], sil[sb], ALU.mult,
                            [self.pres[ub], silres[sb]], [hidres[hs][c][t]])

        def down(gi):
            s = gi % NS
            hs = gi % 2
            for t in range(NT):
                tsl = slice(t * TT, (t + 1) * TT)
                for d in range(KD):
                    db = 4 + cnt["dn"] % 3
                    cnt["dn"] += 1
                    dsl = slice(d * 128, (d + 1) * 128)
                    self.MM(ps[:, db, :], [(wd[s][:, c, dsl], hid[hs][:, c, tsl]) for c in range(G)],
                            [hidres[hs][c][t] for c in range(G)] + [wres[s]], [self.pres[db]])
                    self.STT("dve", self.h[:, d, tsl], ps[:, db, :], 0.5, self.h[:, d, tsl], ALU.mult, ALU.add,
                             [self.pres[db], self.hres[d][t]], [self.hres[d][t]])

        load(0)
        load(1)
        up(0)
        for gi in range(NG):
            if gi + 2 < NG:
                load(gi + 2)
            if gi + 1 < NG:
                up(gi + 1)
            down(gi)
        A.pop()


    def dv(self, t, off, *dims):
        return bass.AP(t, off, [[a, b] for (a, b) in dims])

    def bank(self, lo=0, hi=6):
        key = (lo, hi)
        c = self._bk.get(key, 0)
        self._bk[key] = c + 1
        return lo + c % (hi - lo)

    _bk = {}

    def setup_mixer(self):
        nc, S, A = self.nc, self.S, self.A
        self._bk = {}
        KSZ, VSZ = 64 * 2048, 128 * 1040
        self.KSZ, self.VSZ = KSZ, VSZ
        self.gch = []
        self.gloc = {}

        def add_chunk(entries):
            n = sum(sz for (_, sz) in entries)
            assert n % 512 == 0 and n // 512 <= 1024
            ci = len(self.gch)
            gi = nc.dram_tensor("gin%d" % ci, [n // 512, 512], BF16)
            go = nc.dram_tensor("gout%d" % ci, [4 * (n // 512), 512], BF16)
            self.gch.append((gi, go, n))
            o = 0
            for key, sz in entries:
                self.gloc[key] = (ci, o)
                o += sz
        for sec in ("dK", "mK"):
            for h0 in (0, 4):
                add_chunk([((sec, h), KSZ) for h in range(h0, h0 + 4)])
        self.vgroups = [(0, 3), (3, 6), (6, 8)]
        for sec in ("dV", "mV"):
            for (h0, h1) in self.vgroups:
                add_chunk([((sec, h), VSZ) for h in range(h0, h1)])
        add_chunk([(("mR", 0), 32 * 2048), (("tl", 0), 64 * 512)])
        self.rg_in = [Res("gin%d" % i) for i in range(len(self.gch))]
        self.rg_out = [Res("gout%d" % i) for i in range(len(self.gch))]
        self.mla_chunks = [self.gloc[k][0] for k in (("mR", 0), ("mK", 0), ("mV", 0), ("mV", 3), ("mK", 4), ("mV", 6))]
        self.diff_chunks = [self.gloc[k][0] for k in (("dK", 0), ("dV", 0), ("dV", 3), ("dK", 4), ("dV", 6))]
        dbgk = "ExternalOutput" if self.stage in (1.5, 2.5, 2.7, 3) else "Internal"
        self.q_mla_d = nc.dram_tensor("q_mla_d", [8, 128, 2048], BF16, kind=dbgk)
        self.q_diff_d = nc.dram_tensor("q_diff_d", [8, 128, 2048], BF16, kind=dbgk)
        self.o_d = nc.dram_tensor("o_d", [3, 512, 2048], BF16, kind=dbgk)
        self.p_d = nc.dram_tensor("p_d", [16, 128, 512], BF16, kind=dbgk)
        self.kaug_d = nc.dram_tensor("kaug_d", [2, 32, 8192], BF16, kind=dbgk)
        self.kaugo_d = nc.dram_tensor("kaugo_d", [2, 32, 2048], BF16, kind=dbgk)
        if dbgk == "ExternalOutput":
            self.gdbg = nc.dram_tensor("gdbg", [1024, 512], BF16, kind="ExternalOutput")
        self.r_gin = Res("gin")
        self.r_gout = Res("gout")
        self.r_qm = Res("qm")
        self.r_qd = Res("qd")
        self.r_od = Res("od")
        self.r_pd = Res("pd")
        self.r_kaug = Res("kaug")
        self.cc_sems = [[nc.alloc_semaphore("cc%d_%d" % (l, i)) for i in range(len(self.gch))] for l in range(DEPTH)]
        self.ccdummy = A.alloc([16], F32)

        self.cos_t = A.alloc([T], F32)
        self.sin_t = A.alloc([T], F32)
        self.r_rope = Res("rope")
        self.tri = A.alloc([4, 512], BF16)
        self.ident = A.alloc([128], BF16)
        self.r_tri = Res("tri")
        self.tab2 = A.alloc([16], F32)
        self.r_tab2 = Res("tab2")
        self.cslot = S.slot("cslot")
        self.DMA("pool", self.cslot, [(self.tri, self.c_tri[:, :, :]), (self.ident, self.c_ident[:, :])],
                 writes=[self.r_tri])

    def setup_mixer_compute(self):
        nc, S, A = self.nc, self.S, self.A
        A.push()
        P32 = slice(64, 96)
        posi = A.alloc([T], I32)
        ang = A.alloc([T], F32)
        kf = A.alloc([T], F32)
        ki = A.alloc([T], I32)
        r_t = Res("ropetmp")
        self.DMA("sp", S.slot("ldpos"), [(posi[P32], self.c_pos_own[:, :])], writes=[r_t])
        R = [r_t, self.gres]
        TWO_PI = 2.0 * math.pi

        def fold(x):
            self.TS("dve", kf[P32], x, math.pi, -TWO_PI, ALU.is_gt, ALU.mult, R, R)
            self.TT("dve", x, x, kf[P32], ALU.add, R, R)
            self.TS("dve", x, x, -3.14159, 3.14159, ALU.max, ALU.min, R, R)

        self.CP("dve", ang[P32], posi[P32], R, R)
        self.TS("dve", ang[P32], ang[P32], self.ptab[P32, 16:17], None, ALU.mult, None, R, R)
        self.TS("dve", kf[P32], ang[P32], 1.0 / TWO_PI, None, ALU.mult, None, R, R)
        self.CP("dve", ki[P32], kf[P32], R, R)
        self.CP("dve", kf[P32], ki[P32], R, R)
        self.STT("dve", ang[P32], kf[P32], -6.28125, ang[P32], ALU.mult, ALU.add, R, R)
        self.STT("dve", ang[P32], kf[P32], -0.0019353071795864769, ang[P32], ALU.mult, ALU.add, R, R)
        fold(ang[P32])
        self.ACT(self.sin_t[P32], ang[P32], AF.Sin, R, [self.r_rope])
        self.TS("dve", self.sin_t[P32], self.sin_t[P32], self.ptab[P32, 17:18], None, ALU.mult, None,
                [self.r_rope], [self.r_rope])
        self.TS("dve", ang[P32], ang[P32], math.pi / 2, None, ALU.add, None, R, R)
        fold(ang[P32])
        self.ACT(self.cos_t[P32], ang[P32], AF.Sin, R + [self.r_rope], [self.r_rope])
        A.pop()

        A.push()

        def digits(pos_in, n, tag):
            pi_ = A.alloc([n], I32)
            pf = A.alloc([n], F32)
            af = A.alloc([n], F32)
            bf = A.alloc([n], F32)
            mf = A.alloc([n], F32)
            ai = A.alloc([n], I32)
            rr = [Res("dig" + tag)]
            self.DMA("sp", S.slot("lddig" + tag), [(pi_, pos_in[:, :])], writes=rr)
            self.CP("dve", pf, pi_, rr, rr)
            self.TS("dve", af, pf, 1.0 / 128, None, ALU.mult, None, rr, rr)
            self.CP("dve", ai, af, rr, rr)
            self.CP("dve", af, ai, rr, rr)
            self.STT("dve", bf, af, -128.0, pf, ALU.mult, ALU.add, rr, rr)
            self.TS("dve", mf, bf, 0.0, None, ALU.is_lt, None, rr, rr)
            self.TT("dve", af, af, mf, ALU.subtract, rr, rr)
            self.STT("dve", bf, mf, 128.0, bf, ALU.mult, ALU.add, rr, rr)
            return af, bf, rr

        ak, bk, rk = digits(self.c_pos_g64, 64, "k")
        ao, bo, ro = digits(self.c_pos_own16, 16, "o")
        fl = A.alloc([4, 64], F32)
        r_fl = Res("fl")
        self.DMA("sp", S.slot("ldfl"), [(fl[:, j, :], self.c_flags[j]) for j in range(4)], writes=[r_fl])

        KA = A.alloc([32, 64], BF16)
        KAo = A.alloc([32, 16], BF16)
        r_ka = Res("KA")
        kslot = S.slot("kslot")
        for v in range(2):
            self.MEMSET("dve", KA, 0.0, [r_ka])
            self.MEMSET("dve", KAo, 0.0, [r_ka])
            if v == 0:
                self.CP("dve", KA[:, 0, :], ak, rk + [r_ka], [r_ka])
                self.CP("dve", KA[:, 1, :], bk, rk + [r_ka], [r_ka])
                self.MEMSET("dve", KA[:, 2:4, :], 1.0, [r_ka])
                self.CP("dve", KAo[:, 0, :], ao, ro + [r_ka], [r_ka])
                self.CP("dve", KAo[:, 1, :], bo, ro + [r_ka], [r_ka])
                self.MEMSET("dve", KAo[:, 2:4, :], 1.0, [r_ka])
                f0 = 4
            else:
                f0 = 0
            self.CP("dve", KA[:, f0:f0 + 4, :], fl, [r_fl, r_ka], [r_ka])
            self.DMA("sp", kslot, [
                (self.dv(self.kaug_d, v * 32 * 8192, (64, 128), (8192, 32), (1, 64)), KA),
                (self.dv(self.kaugo_d, v * 32 * 2048, (16, 128), (2048, 32), (1, 16)), KAo)],
                reads=[r_ka, self.r_kaug])

        QA = A.alloc([32, 16], BF16)
        r_qa = Res("QA")
        qslot = S.slot("qslot")
        tmpq = A.alloc([16], F32)
        for hh in range(9):
            self.MEMSET("dve", QA, 0.0, [r_qa])
            f0 = 4 if hh < 8 else 0
            for j in range(4):
                self.MEMSET("dve", QA[32 * j:32 * j + 32, f0 + j, :], -30000.0, [r_qa])
            if hh < 8:
                sl = 2.0 ** (-(hh + 1))
                self.MEMSET("dve", QA[:, 0, :], 128.0 * sl, [r_qa])
                self.MEMSET("dve", QA[:, 1, :], sl, [r_qa])
                self.TS("dve", QA[:, 2, :], ao, -128.0 * sl, None, ALU.mult, None, ro + [r_qa], [r_qa])
                self.TS("dve", QA[:, 3, :], bo, -sl, None, ALU.mult, None, ro + [r_qa], [r_qa])
                self.DMA("sp", qslot, [
                    (self.dv(self.q_diff_d, hh * 128 * 2048 + r0 * 2048, (16, 128), (2048, 32), (1, 16)), QA)
                    for r0 in (0, 96)], reads=[r_qa, self.r_qd])
            else:
                self.DMA("sp", qslot, [
                    (self.dv(self.q_mla_d, h2 * 128 * 2048 + 96 * 2048, (16, 128), (2048, 32), (1, 16)), QA)
                    for h2 in range(8)], reads=[r_qa, self.r_qm])

        lam = A.alloc([8, 32], F32)
        r_lam = Res("lam")
        self.DMA("sp", S.slot("ldlam"), [(lam, self.c_lam[:, :, :])], writes=[r_lam])
        prod = A.alloc([32], F32)
        e12 = A.alloc([4], F32)
        for l in range(DEPTH):
            lam_init = 0.8 - 0.6 * math.exp(-0.3 * l)
            for i in range(2):
                self.TT("dve", prod, lam[:, l * 4 + 2 * i, :], lam[:, l * 4 + 2 * i + 1, :], ALU.mult, [r_lam], [r_lam])
                self.S.op("dve", (lambda e, o=e12[:, i:i + 1], p=prod: e.reduce_sum(o, p, mybir.AxisListType.X)),
                          [r_lam], [r_lam])
                self.ACT(e12[:, i:i + 1], e12[:, i:i + 1], AF.Exp, [r_lam], [r_lam])
            self.TT("dve", e12[:, 2:3], e12[:, 1:2], e12[:, 0:1], ALU.subtract, [r_lam], [r_lam])
            self.TS("dve", self.tab2[:, l:l + 1], e12[:, 2:3], -lam_init, None, ALU.add, None, [r_lam], [self.r_tab2])
            self.TS("dve", self.tab2[:, 2 + l:3 + l], self.gains[:, 62 + l:63 + l], 1.0 - lam_init, None, ALU.mult, None,
                    [self.gres], [self.r_tab2])
        A.pop()

    def mixer(self, l):
        self.phaseA(l)
        self.phaseB(l)
        if self.stage == 3:
            r = Res("dbg3")
            self.DMA("sp", self.S.slot("dbg3"), [(self.gdbg.ap()[0:8, :], self.gch[0][0].ap()[0:8, :])],
                     writes=[self.r_qm, self.r_qd, self.r_od, self.r_pd, self.r_kaug, r])
            self.dbg_res = [r]
            return
        self.phaseC0(l)
        self.phaseC1(l)

    def gin_ap(self, sec, h, off, *dims):
        ci, o = self.gloc[(sec, h)]
        return self.dv(self.gch[ci][0], o + off, *dims)

    def gout_ap(self, r, sec, h, off, *dims):
        ci, o = self.gloc[(sec, h)]
        return self.dv(self.gch[ci][1], r * self.gch[ci][2] + o + off, *dims)

    def gather_chunks(self, l, chunks):
        groups = [[0, 1, 2, 3], [4, 5, 6, 7]]
        for i in chunks:
            gi, go, n = self.gch[i]
            self.S.custom("pool", (lambda e, gi=gi, go=go: e.collective_compute(
                "AllGather", ALU.bypass, replica_groups=groups, ins=[gi.ap().opt()], outs=[go.ap().opt()])),
                self.cc_sems[l][i], 1, reads=[], writes=[self.rg_in[i], self.rg_out[i]])

    def ci(self, sec, h=0):
        return self.gloc[(sec, h)][0]

    def phaseA(self, l):
        A, S = self.A, self.S
        ps = self.psum
        A.push()
        u = A.alloc([KD, T], BF16)
        ures = [[Res("u") for t in range(NT)] for k in range(KD)]
        sq = [A.alloc([TT], BF16) for _ in range(2)]
        sqres = [Res("sq") for _ in range(2)]
        rs = [A.alloc([TT], F32) for _ in range(2)]
        rsres = [Res("rs") for _ in range(2)]
        self.rmsnorm_fm(self.h, self.hres, u, ures, 16 + l * 8, KD, D, sq, sqres, rs, rsres, stat_bank=7)

        win = self.w["w_in"].ap()[l].rearrange("(k p) c -> p k c", p=128)
        NW = 2
        wb = [A.alloc([KD, 576], BF16) for _ in range(NW)]
        wbres = [Res("wb") for _ in range(NW)]
        wbslot = [S.slot() for _ in range(NW)]
        for i in range(NW):
            self.MEMSET("pool", wb[i][:, :, 544:576], 0.0, [wbres[i]])
        self._wbi = 0

        def wload(pairs_fn):
            i = self._wbi % NW
            self._wbi += 1
            self.DMA("pool", wbslot[i], pairs_fn(wb[i]), writes=[wbres[i]])
            return wb[i], wbres[i]

        uq = A.alloc([2, 8, 96], BF16)
        uqr = A.alloc([2, 8, 96], BF16)
        ukvk = A.alloc([8, 64], BF16)
        ukvv = A.alloc([8, 64], BF16)
        r_sw = Res("smallw")
        swslot = S.slot()
        uqv = self.w["mla_w_uq"].ap()[l].rearrange("(k p) (h c) -> p k h c", p=128, c=96)
        ukvv_d = self.w["mla_w_ukv"].ap()[l].rearrange("p (h c) -> p h c", c=128)
        self.MEMSET("pool", uqr, 0.0, [r_sw])
        swp = [(ukvk, ukvv_d[:, :, 0:64]), (ukvv, ukvv_d[:, :, 64:128])]
        for k2 in range(2):
            swp += [(uq[:, k2, :, :], uqv[:, k2, :, :]),
                    (uqr[:, k2, :, 64:80], uqv[:, k2, :, 80:96]), (uqr[:, k2, :, 80:96], uqv[:, k2, :, 64:80])]
        self.DMA("pool", swslot, swp, writes=[r_sw])

        stg = [A.alloc([8, TT], BF16) for _ in range(2)]
        stgres = [Res("stg") for _ in range(2)]
        stgslot = [S.slot() for _ in range(2)]
        vstg = [A.alloc([8, 4, 65], BF16) for _ in range(2)]
        vstgres = [Res("vstg") for _ in range(2)]
        vstgslot = [S.slot() for _ in range(2)]
        for i in range(2):
            self.MEMSET("pool", vstg[i][:, :, :, 64:65], 1.0, [vstgres[i]])
        pstg = [A.alloc([4, 512], BF16) for _ in range(2)]
        pstgres = [Res("pstg") for _ in range(2)]
        pstgslot = [S.slot() for _ in range(2)]
        cq = A.alloc([2, TT], F32)
        cqn = A.alloc([2, TT], BF16)
        r_cq = [Res("cq0"), Res("cq1")]
        r_cqn = [Res("cqn0"), Res("cqn1")]
        ckv = A.alloc([TT], F32)
        ckvn = A.alloc([TT], BF16)
        r_ckv, r_ckvn = Res("ckv"), Res("ckvn")
        t1 = A.alloc([TT], F32)
        t2 = A.alloc([TT], F32)
        r_t1, r_t2 = Res("t1"), Res("t2")
        rstg = A.alloc([TT], BF16)
        r_rstg = Res("rstg")
        rslot = S.slot()
        cnt = {"stg": 0, "v": 0, "p": 0}
        P32 = slice(64, 96)

        def urow(t):
            return [ures[k][t] for k in range(KD)]

        def rope(dst, x_ps, xs_ps, tsl, scale, reads, wres_):
            self.STT("dve", t1[P32], x_ps, scale, self.cos_t[P32, tsl], ALU.mult, ALU.mult,
                     reads + [self.r_rope], [r_t1])
            self.STT("dve", t2[P32], xs_ps, scale, self.sin_t[P32, tsl], ALU.mult, ALU.mult,
                     reads + [self.r_rope], [r_t2])
            self.TT("dve", dst, t1[P32], t2[P32], ALU.add, [r_t1, r_t2], wres_)

        wq, rwq = wload(lambda b: [(b[:, :, 0:256], win[:, :, 0:256])])
        for t in range(NT):
            tsl = slice(t * TT, (t + 1) * TT)
            for c2 in range(2):
                bk = self.bank()
                self.MM(ps[:, bk, :], [(wq[:, k, c2 * 128:(c2 + 1) * 128], u[:, k, tsl]) for k in range(KD)],
                        urow(t) + [rwq], [self.pres[bk]])
                self.ACT(cq[:, c2, :], ps[:, bk, :], AF.Copy, [self.pres[bk]], [r_cq[c2]])
            self.rmsnorm_tile([cq[:, 0, :], cq[:, 1, :]], r_cq, [cqn[:, 0, :], cqn[:, 1, :]], r_cqn,
                              [self.gains[:, 56 + 2 * l + c2:57 + 2 * l + c2] for c2 in range(2)], 256,
                              sq, sqres, rs, rsres, stat_bank=7)
            si = cnt["stg"] % 2
            cnt["stg"] += 1
            for hh in range(8):
                b1 = self.bank()
                b2 = self.bank()
                self.MM(ps[0:96, b1, :], [(uq[:, k2, hh, :], cqn[:, k2, :]) for k2 in range(2)],
                        r_cqn + [r_sw], [self.pres[b1]])
                self.MM(ps[0:96, b2, :], [(uqr[:, k2, hh, :], cqn[:, k2, :]) for k2 in range(2)],
                        r_cqn + [r_sw], [self.pres[b2]])
                self.ACT(stg[si][0:64, hh, :], ps[0:64, b1, :], AF.Copy, [self.pres[b1]], [stgres[si]],
                         scale=SC_MLA)
                rope(stg[si][64:96, hh, :], ps[64:96, b1, :], ps[64:96, b2, :], tsl, SC_MLA,
                     [self.pres[b1], self.pres[b2]], [stgres[si]])
            self.DMA("sp", stgslot[si], [(self.dv(self.q_mla_d, t * TT, (2048, 96), (128 * 2048, 8), (1, TT)),
                                          stg[si][0:96, :, :])], reads=[stgres[si], self.r_qm])

        wk, rwk = wload(lambda b: [(b[:, :, 0:160], win[:, :, 256:416]), (b[:, :, 160:176], win[:, :, 400:416]),
                                   (b[:, :, 176:192], win[:, :, 384:400])])
        for t in range(NT):
            tsl = slice(t * TT, (t + 1) * TT)
            bk = self.bank()
            self.MM(ps[:, bk, :], [(wk[:, k, 0:128], u[:, k, tsl]) for k in range(KD)], urow(t) + [rwk],
                    [self.pres[bk]])
            self.ACT(ckv, ps[:, bk, :], AF.Copy, [self.pres[bk]], [r_ckv])
            self.rmsnorm_tile([ckv], [r_ckv], [ckvn], [r_ckvn], [self.gains[:, 60 + l:61 + l]], 128,
                              sq, sqres, rs, rsres, stat_bank=7)
            si = cnt["stg"] % 2
            cnt["stg"] += 1
            for hh in range(8):
                bk = self.bank()
                self.MM(ps[0:64, bk, :], [(ukvk[:, hh, :], ckvn)], [r_ckvn, r_sw], [self.pres[bk]])
                if hh % 2:
                    self.CP("dve", stg[si][0:64, hh, :], ps[0:64, bk, :], [self.pres[bk]], [stgres[si]])
                else:
                    self.ACT(stg[si][0:64, hh, :], ps[0:64, bk, :], AF.Copy, [self.pres[bk]], [stgres[si]])
            self.DMA("sp", stgslot[si], [(self.gin_ap("mK", h0, t * TT, (2048, 64), (64 * 2048, 4), (1, TT)),
                                          stg[si][0:64, h0:h0 + 4, :]) for h0 in (0, 4)],
                     reads=[stgres[si], self.rg_in[self.ci("mK", 0)], self.rg_in[self.ci("mK", 4)]])
            vi = cnt["v"] % 2
            cnt["v"] += 1
            for blk in range(4):
                bk = self.bank()
                self.MM(ps[:, bk, :], [(ckvn[:, blk * 128:(blk + 1) * 128], ukvv.rearrange("p h c -> p (h c)"))],
                        [r_ckvn, r_sw], [self.pres[bk]])
                self.CP("dve", vstg[vi][:, :, blk, 0:64], ps[:, bk, :].rearrange("p (h c) -> p h c", c=64),
                        [self.pres[bk]], [vstgres[vi]])
            self.DMA("sp", vstgslot[vi], [(self.gin_ap("mV", h0, t * 260, (1040, 128), (128 * 1040, h1 - h0), (1, 260)),
                                           vstg[vi][:, h0:h1].rearrange("p h b c -> p h (b c)"))
                                          for (h0, h1) in self.vgroups],
                     reads=[vstgres[vi]] + [self.rg_in[self.ci("mV", h0)] for (h0, h1) in self.vgroups])
            b1 = self.bank()
            b2 = self.bank()
            self.MM(ps[0:96, b1, :], [(wk[:, k, 64:160], u[:, k, tsl]) for k in range(KD)], urow(t) + [rwk],
                    [self.pres[b1]])
            self.MM(ps[0:96, b2, :], [(wk[:, k, 96:192], u[:, k, tsl]) for k in range(KD)], urow(t) + [rwk],
                    [self.pres[b2]])
            rope(rstg[P32], ps[64:96, b1, :], ps[64:96, b2, :], tsl, 1.0, [self.pres[b1], self.pres[b2]], [r_rstg])
            self.DMA("sp", rslot, [(self.gin_ap("mR", 0, t * TT, (2048, 32), (1, TT)), rstg[P32])],
                     reads=[r_rstg, self.rg_in[self.ci("mR")]])

        for which in (1,):
            c0 = 1440 + which * 512
            wv_, rwv = wload(lambda b, c0=c0: [(b[:, :, 0:512], win[:, :, c0:c0 + 512])])
            for t in range(NT):
                if which == 0:
                    vi = cnt["v"] % 2
                    cnt["v"] += 1
                else:
                    pi_ = cnt["p"] % 2
                    cnt["p"] += 1
                for blk in range(4):
                    tok = slice(t * TT + blk * 128, t * TT + (blk + 1) * 128)
                    bk = self.bank()
                    self.MM(ps[:, bk, :], [(u[:, k, tok], wv_[:, k, 0:512]) for k in range(KD)], urow(t) + [rwv],
                            [self.pres[bk]])
                    if which == 0:
                        pv3 = ps[:, bk, :].rearrange("p (h c) -> p h c", c=64)
                        if blk % 2:
                            self.CP("dve", vstg[vi][:, :, blk, 0:64], pv3, [self.pres[bk]], [vstgres[vi]])
                        else:
                            self.ACT(vstg[vi][:, :, blk, 0:64], pv3, AF.Copy, [self.pres[bk]], [vstgres[vi]])
                    else:
                        if blk % 2:
                            self.CP("dve", pstg[pi_][:, blk, :], ps[:, bk, :], [self.pres[bk]], [pstgres[pi_]])
                        else:
                            self.ACT(pstg[pi_][:, blk, :], ps[:, bk, :], AF.Copy, [self.pres[bk]], [pstgres[pi_]])
                if which == 0:
                    self.DMA("sp", vstgslot[vi], [(self.gin_ap("dV", h0, t * 260, (1040, 128), (128 * 1040, h1 - h0),
                                                               (1, 260)),
                                                   vstg[vi][:, h0:h1].rearrange("p h b c -> p h (b c)"))
                                                  for (h0, h1) in self.vgroups],
                             reads=[vstgres[vi]] + [self.rg_in[self.ci("dV", h0)] for (h0, h1) in self.vgroups])
                else:
                    self.DMA("sp", pstgslot[pi_], [
                        (self.dv(self.p_d, t * 4 * 128 * 512, (512, 128), (128 * 512, 4), (1, 512)), pstg[pi_]),
                        (self.gin_ap("tl", 0, t * 16 * 512, (512, 16), (1, 512)), pstg[pi_][112:128, 3, :])],
                        reads=[pstgres[pi_], self.rg_in[self.ci("tl")], self.r_pd])
        self.gather_chunks(l, self.mla_chunks)
        for which in range(2):
            c0 = 416 + which * 512
            wd_, rwd = wload(lambda b, c0=c0: [(b[:, :, 32:544], win[:, :, c0:c0 + 512])])
            for t in range(NT):
                tsl = slice(t * TT, (t + 1) * TT)
                si = cnt["stg"] % 2
                cnt["stg"] += 1
                for hh in range(8):
                    bk = self.bank()
                    self.MM(ps[:, bk, :], [(wd_[:, k, hh * 64:hh * 64 + 128], u[:, k, tsl]) for k in range(KD)],
                            urow(t) + [rwd], [self.pres[bk]])
                    sc_ = SC_DIFF if which == 0 else 1.0
                    self.TS("dve", stg[si][32:64, hh, :], ps[32:64, bk, :], sc_, None, ALU.mult, None,
                            [self.pres[bk]], [stgres[si]])
                    self.ACT(stg[si][64:96, hh, :], ps[64:96, bk, :], AF.Copy, [self.pres[bk]],
                             [stgres[si]], scale=sc_)
                if which == 0:
                    self.DMA("sp", stgslot[si], [(self.dv(self.q_diff_d, 32 * 2048 + t * TT, (2048, 64), (128 * 2048, 8),
                                                          (1, TT)), stg[si][32:96, :, :])],
                             reads=[stgres[si], self.r_qd])
                else:
                    self.DMA("sp", stgslot[si], [(self.gin_ap("dK", h0, t * TT, (2048, 64), (64 * 2048, 4), (1, TT)),
                                                  stg[si][32:96, h0:h0 + 4, :]) for h0 in (0, 4)],
                             reads=[stgres[si], self.rg_in[self.ci("dK", 0)], self.rg_in[self.ci("dK", 4)]])

        for which in (0,):
            c0 = 1440 + which * 512
            wv_, rwv = wload(lambda b, c0=c0: [(b[:, :, 0:512], win[:, :, c0:c0 + 512])])
            for t in range(NT):
                if which == 0:
                    vi = cnt["v"] % 2
                    cnt["v"] += 1
                else:
                    pi_ = cnt["p"] % 2
                    cnt["p"] += 1
                for blk in range(4):
                    tok = slice(t * TT + blk * 128, t * TT + (blk + 1) * 128)
                    bk = self.bank()
                    self.MM(ps[:, bk, :], [(u[:, k, tok], wv_[:, k, 0:512]) for k in range(KD)], urow(t) + [rwv],
                            [self.pres[bk]])
                    if which == 0:
                        pv3 = ps[:, bk, :].rearrange("p (h c) -> p h c", c=64)
                        if blk % 2:
                            self.CP("dve", vstg[vi][:, :, blk, 0:64], pv3, [self.pres[bk]], [vstgres[vi]])
                        else:
                            self.ACT(vstg[vi][:, :, blk, 0:64], pv3, AF.Copy, [self.pres[bk]], [vstgres[vi]])
                    else:
                        if blk % 2:
                            self.CP("dve", pstg[pi_][:, blk, :], ps[:, bk, :], [self.pres[bk]], [pstgres[pi_]])
                        else:
                            self.ACT(pstg[pi_][:, blk, :], ps[:, bk, :], AF.Copy, [self.pres[bk]], [pstgres[pi_]])
                if which == 0:
                    self.DMA("sp", vstgslot[vi], [(self.gin_ap("dV", h0, t * 260, (1040, 128), (128 * 1040, h1 - h0),
                                                               (1, 260)),
                                                   vstg[vi][:, h0:h1].rearrange("p h b c -> p h (b c)"))
                                                  for (h0, h1) in self.vgroups],
                             reads=[vstgres[vi]] + [self.rg_in[self.ci("dV", h0)] for (h0, h1) in self.vgroups])
                else:
                    self.DMA("sp", pstgslot[pi_], [
                        (self.dv(self.p_d, t * 4 * 128 * 512, (512, 128), (128 * 512, 4), (1, 512)), pstg[pi_]),
                        (self.gin_ap("tl", 0, t * 16 * 512, (512, 16), (1, 512)), pstg[pi_][112:128, 3, :])],
                        reads=[pstgres[pi_], self.rg_in[self.ci("tl")], self.r_pd])
        self.gather_chunks(l, self.diff_chunks)
        A.pop()


    def phaseB(self, l):
        A, S = self.A, self.S
        ps = self.psum
        A.push()
        Kb = [A.alloc([8192], BF16) for _ in range(2)]
        Ko = [A.alloc([2048], BF16) for _ in range(2)]
        Vb = [A.alloc([64, 65], BF16) for _ in range(2)]
        Vo = [A.alloc([16, 65], BF16) for _ in range(2)]
        Qb = [A.alloc([2048], BF16) for _ in range(2)]
        inres = [Res("in0"), Res("in1")]
        inslot = [S.slot(), S.slot()]
        augres = [Res("aug0"), Res("aug1")]
        augslot = [S.slot(), S.slot()]
        NPT = 5
        PT = [A.alloc([2, TT], BF16) for _ in range(NPT)]
        ptres = [Res("pt") for _ in range(NPT)]
        osb = [A.alloc([TT], F32) for _ in range(2)]
        r_osb = [Res("osb0"), Res("osb1")]
        rden = [A.alloc([TT], F32) for _ in range(2)]
        r_rden = [Res("rden0"), Res("rden1")]
        od = A.alloc([TT], F32)
        od2 = A.alloc([TT], F32)
        r_od, r_od2 = Res("odt"), Res("od2t")
        sq = [A.alloc([TT], BF16) for _ in range(1)]
        sqres = [Res("sqb")]
        rs = [A.alloc([TT], F32) for _ in range(1)]
        rsres = [Res("rsb")]
        ostg = [A.alloc([TT], BF16) for _ in range(2)]
        ostgres = [Res("ostg0"), Res("ostg1")]
        ostgslot = [S.slot(), S.slot()]
        cnt = {"pt": 0, "o": 0, "bp": 0}
        hc = 0
        for kind in (0, 1):
            for s_ in range(2):
                pairs = []
                if kind == 0:
                    pairs.append((Kb[s_][96:128, :], self.dv(self.kaug_d, 1 * 32 * 8192, (8192, 32), (1, 8192))))
                    pairs.append((Ko[s_][96:128, :], self.dv(self.kaugo_d, 1 * 32 * 2048, (2048, 32), (1, 2048))))
                    for r in range(4):
                        pairs.append((Kb[s_][64:96, r * 2048:(r + 1) * 2048],
                                      self.gout_ap(r, "mR", 0, 0, (2048, 32), (1, 2048))))
                    pairs.append((Ko[s_][64:96, :], self.gin_ap("mR", 0, 0, (2048, 32), (1, 2048))))
                else:
                    for r0 in (0, 96):
                        pairs.append((Kb[s_][r0:r0 + 32, :], self.dv(self.kaug_d, 0, (8192, 32), (1, 8192))))
                        pairs.append((Ko[s_][r0:r0 + 32, :], self.dv(self.kaugo_d, 0, (2048, 32), (1, 2048))))
                self.DMA("sp", augslot[s_], pairs, writes=[augres[s_], inres[s_], self.r_kaug,
                                                            self.rg_in[self.ci("mR")], self.rg_out[self.ci("mR")]])
            def head_loads(hh, s_):
                ksec = "mK" if kind == 0 else "dK"
                vsec = "mV" if kind == 0 else "dV"
                qd = self.q_mla_d if kind == 0 else self.q_diff_d
                pairs = []
                kr0 = 0 if kind == 0 else 32
                for r in range(4):
                    pairs.append((Kb[s_][kr0:kr0 + 64, r * 2048:(r + 1) * 2048],
                                  self.gout_ap(r, ksec, hh, 0, (2048, 64), (1, 2048))))
                    pairs.append((Vb[s_][:, r * 16:(r + 1) * 16, :].rearrange("p b c -> p (b c)"),
                                  self.gout_ap(r, vsec, hh, 0, (1040, 128), (1, 1040))))
                pairs.append((Ko[s_][kr0:kr0 + 64, :], self.gin_ap(ksec, hh, 0, (2048, 64), (1, 2048))))
                pairs.append((Vo[s_].rearrange("p b c -> p (b c)"), self.gin_ap(vsec, hh, 0, (1040, 128), (1, 1040))))
                pairs.append((Qb[s_], self.dv(qd, hh * 128 * 2048, (2048, 128), (1, 2048))))
                ck, cv = self.ci(ksec, hh), self.ci(vsec, hh)
                self.DMA("sp", inslot[s_], pairs,
                         writes=[inres[s_], self.rg_in[ck], self.rg_out[ck], self.rg_in[cv], self.rg_out[cv],
                                 self.r_qm if kind == 0 else self.r_qd])

            nmap = 1 if kind == 0 else 2
            tiles = []
            for hh in range(8):
                s_ = (hc + hh) % 2
                K, KO, V, VO, Q = Kb[s_], Ko[s_], Vb[s_], Vo[s_], Qb[s_]
                for j in range(NT):
                    steps = []
                    for j2 in range(j + 1):
                        for r in range(4):
                            if j2 == j and r == (3 if j % 2 == 0 else 0):
                                continue
                            for m in range(4):
                                c0 = r * 2048 + j2 * 512 + m * 128
                                steps.append((K[:, c0:c0 + 128], V[:, r * 16 + j2 * 4 + m, :], None))
                    for m in range(4):
                        c0 = j * 512 + m * 128
                        steps.append((KO[:, c0:c0 + 128], VO[:, j * 4 + m, :], m))
                    n = len(steps)
                    if kind == 0:
                        units = [((2 * u_, 0), (2 * u_ + 1, 0)) for u_ in range(n // 2)]
                    else:
                        units = [((u_, 0), (u_, 1)) for u_ in range(n)]
                    tiles.append(dict(hh=hh, j=j, s=s_, steps=steps, units=units, Q=Q,
                                      rin=[inres[s_], augres[s_]]))
            glist = [(ti, ui) for ti, tl_ in enumerate(tiles) for ui in range(len(tl_["units"]))]
            ubank = {}

            def smm(g):
                ti, ui = glist[g]
                tl_ = tiles[ti]
                qsl = slice(tl_["j"] * TT, (tl_["j"] + 1) * TT)
                bp = freep.pop(0)
                ubank[g] = bp
                for half, (si_, mp) in enumerate(tl_["units"][ui]):
                    kk, vv, dm = tl_["steps"][si_]
                    rows = slice(0, 128) if kind == 0 else slice(64 * mp, 64 * mp + 64)
                    pr = [(kk[rows], tl_["Q"][rows, qsl])]
                    rd = list(tl_["rin"])
                    if dm is not None:
                        pr.append((self.ident[:, :], self.tri[:, dm, :]))
                        rd.append(self.r_tri)
                    self.MM(ps[:, bp + half, :], pr, rd, [self.pres[bp + half]])

            def pv(g):
                ti, ui = glist[g]
                tl_ = tiles[ti]
                n = len(tl_["steps"])
                bp = ubank.pop(g)
                pi_ = cnt["pt"] % NPT
                cnt["pt"] += 1
                self.ACT(PT[pi_], ps[:, bp:bp + 2, :], AF.Exp, [self.pres[bp], self.pres[bp + 1]], [ptres[pi_]])
                freep.append(bp)

                def pvmm(half, si_, mp):
                    kk, vv, dm = tl_["steps"][si_]
                    ob = obank(ti, mp)
                    self.MM(ps[0:65, ob, :], [(vv, PT[pi_][:, half, :])], [ptres[pi_]] + tl_["rin"], [self.pres[ob]],
                            start=(si_ == 0), stop=(si_ == n - 1))
                (s0, m0), (s1, m1) = tl_["units"][ui]
                pvmm(0, s0, m0)
                if kind == 0:
                    pvmm(1, s1, m1)
                    return None
                return lambda: pvmm(1, s1, m1)

            pending = []

            def obank(ti, mp):
                return 6 + (ti % 2) if kind == 0 else 6 + mp

            def finalize_a(ti):
                ob = obank(ti, 0)
                self.CP("dve", osb[0][0:65], ps[0:65, ob, :], [self.pres[ob]], [r_osb[0]])

            def finalize(ti, g, skip_a=False):
                tl_ = tiles[ti]
                hh, j = tl_["hh"], tl_["j"]
                oi = cnt["o"] % 2
                cnt["o"] += 1
                P64 = slice(0, 64)
                if not skip_a:
                    finalize_a(ti)
                if nmap == 2:
                    ob = obank(ti, 1)
                    self.ACT(osb[1][0:65], ps[0:65, ob, :], AF.Copy, [self.pres[ob]], [r_osb[1]])
                for mp in range(nmap):
                    self.RECIP(rden[mp][64:65], osb[mp][64:65], [r_osb[mp]], [r_rden[mp]])
                if kind == 1:
                    self.TS("dve", rden[1][64:65], rden[1][64:65], self.tab2[64:65, l:l + 1], None, ALU.mult, None,
                            [r_rden[1], self.r_tab2], [r_rden[1]])

                def store():
                    self.DMA("pool", ostgslot[oi], [(self.dv(self.o_d, kind * 512 * 2048 + hh * 64 * 2048 + j * TT,
                                                             (2048, 64), (1, TT)), ostg[oi][0:64])],
                             reads=[ostgres[oi], self.r_od])

                def stage2():
                    bp = freep.pop(0)
                    for mp in range(nmap):
                        self.MM(ps[0:64, bp + mp, :], [(self.ones[64:65, 0:64], rden[mp][64:65])],
                                [r_rden[mp], self.onesres], [self.pres[bp + mp]])
                    if kind == 0:
                        self.TT("dve", ostg[oi][P64], osb[0][P64], ps[P64, bp, :], ALU.mult,
                                [r_osb[0], self.pres[bp]], [ostgres[oi]])
                        store()
                    else:
                        self.TT("dve", od[P64], osb[0][P64], ps[P64, bp, :], ALU.mult, [r_osb[0], self.pres[bp]], [r_od])
                        self.TT("dve", od2[P64], osb[1][P64], ps[P64, bp + 1, :], ALU.mult,
                                [r_osb[1], self.pres[bp + 1]], [r_od2])
                        self.TT("dve", od[P64], od[P64], od2[P64], ALU.add, [r_od, r_od2], [r_od])
                        self.TT("dve", sq[0][P64], od[P64], od[P64], ALU.mult, [r_od], [sqres[0]])
                    freep.append(bp)

                def stage3():
                    bp = freep.pop(0)
                    self.MM(ps[P64, bp, :], [(self.ones_bf[P64, 0:64], sq[0][P64])], [sqres[0], self.onesres],
                            [self.pres[bp]])
                    self.ACT(rs[0][P64], ps[P64, bp, :], AF.Ln, [self.pres[bp]], [rsres[0]], scale=1.0 / 64, bias=EPS)
                    freep.append(bp)
                    self.ACT(rs[0][P64], rs[0][P64], AF.Exp, [rsres[0]], [rsres[0]], scale=-0.5)
                    self.STT("dve", ostg[oi][P64], od[P64], self.tab2[P64, 2 + l:3 + l], rs[0][P64], ALU.mult, ALU.mult,
                             [r_od, rsres[0], self.r_tab2], [ostgres[oi]])
                    store()
                pending.append((g + 5, stage2))
                if kind == 1:
                    pending.append((g + 9, stage3))

            def run_pending(g, flush=False):
                keep = []
                for (due, fn) in pending:
                    if flush or due <= g:
                        fn()
                    else:
                        keep.append((due, fn))
                pending[:] = keep

            freep = [0, 2, 4]
            deferred = [None, None]
            LOOK = 2
            ng = len(glist)
            head_loads(0, hc % 2)
            for g in range(min(LOOK, ng)):
                smm(g)
            for g in range(ng):
                ti, ui = glist[g]
                if g + LOOK < ng:
                    smm(g + LOOK)
                late = pv(g)
                if deferred[0] is not None:
                    deferred[0]()
                    deferred[0] = None
                    if deferred[1] is not None:
                        finalize(deferred[1], g, skip_a=True)
                        deferred[1] = None
                if ui == 0 and tiles[ti]["j"] == 0 and tiles[ti]["hh"] + 1 < 8:
                    head_loads(tiles[ti]["hh"] + 1, (hc + tiles[ti]["hh"] + 1) % 2)
                run_pending(g)
                last = (ui == len(tiles[ti]["units"]) - 1)
                if late is None:
                    if last:
                        finalize(ti, g)
                else:
                    deferred[0] = late
                    deferred[1] = ti if last else None
                    if last:
                        finalize_a(ti)
            if deferred[0] is not None:
                deferred[0]()
                if deferred[1] is not None:
                    finalize(deferred[1], ng, skip_a=True)
            run_pending(ng + 100, flush=True)
            hc += 8
        A.pop()

    def phaseC0(self, l):
        A, S = self.A, self.S
        ps = self.psum
        A.push()
        ptok = A.alloc([16, 512], BF16)
        tails = A.alloc([2, 512], BF16)
        band = A.alloc([3, 4, 128], BF16)
        tsel = A.alloc([2, 4, 4, 128], BF16)
        pw = A.alloc([4, 128], BF16)
        r_in = Res("c0in")
        r_w = Res("c0w")
        sl1, sl2 = S.slot(), S.slot()
        pairs = [(ptok, self.dv(self.p_d, 0, (512, 128), (128 * 512, 16), (1, 512)))]
        for r in range(4):
            pairs.append((tails[(r % 2) * 64:(r % 2) * 64 + 64, r // 2, :],
                          self.gout_ap(r, "tl", 0, 0, (512, 64), (1, 512))))
        self.DMA("sp", sl1, pairs, writes=[r_in, self.r_pd, self.rg_out[self.ci("tl")]])
        pwv = self.w["pool_w"].ap()[l].rearrange("g c d -> c g d")
        self.DMA("pool", sl2, [(band, self.c_band[:, :, :, :]), (tsel, self.c_tailsel[:, :, :, :, :]), (pw, pwv)],
                 writes=[r_w])
        pooled = [A.alloc([TT], BF16) for _ in range(2)]
        r_pooled = [Res("pooled0"), Res("pooled1")]
        ystg = [A.alloc([4, TT], BF16) for _ in range(2)]
        r_ystg = [Res("ystg0"), Res("ystg1")]
        yslot = [S.slot(), S.slot()]
        c = 0
        for t in range(NT):
            yi = t % 2
            for g in range(4):
                gsl = slice(g * 128, (g + 1) * 128)
                bk = self.bank()
                for blk in range(4):
                    bb = t * 4 + blk
                    bm = band[:, 2, g, :] if bb == 0 else band[:, 0, g, :]
                    pr = [(ptok[:, bb, gsl], bm)]
                    if blk == 0:
                        pr.append((tails[:, 0, gsl], tsel[:, 0, t, g, :]))
                        pr.append((tails[:, 1, gsl], tsel[:, 1, t, g, :]))
                    else:
                        pr.append((ptok[:, bb - 1, gsl], band[:, 1, g, :]))
                    self.MM(ps[:, bk, blk * 128:(blk + 1) * 128], pr, [r_in, r_w], [self.pres[bk]])
                pi_ = c % 2
                c += 1
                self.ACT(pooled[pi_], ps[:, bk, :], AF.Copy, [self.pres[bk]], [r_pooled[pi_]])
                b2 = self.bank()
                self.MM(ps[:, b2, :], [(pw[:, g, :], pooled[pi_])], [r_pooled[pi_], r_w], [self.pres[b2]])
                self.TS("dve", ystg[yi][:, g, :], ps[:, b2, :], self.ptab[:, l * 4 + g:l * 4 + g + 1],
                        self.ptab[:, 8 + l * 4 + g:8 + l * 4 + g + 1], ALU.add, ALU.mult, [self.pres[b2]], [r_ystg[yi]])
            self.DMA("sp", yslot[yi], [(self.dv(self.o_d, 2 * 512 * 2048 + t * TT, (2048, 128), (128 * 2048, 4), (1, TT)),
                                        ystg[yi])], reads=[r_ystg[yi], self.r_od])
        A.pop()

    def phaseC1(self, l):
        A, S = self.A, self.S
        ps = self.psum
        A.push()
        HT = 1024
        u = A.alloc([KD, HT], BF16)
        ures = [[Res("uc") for t in range(2)] for k in range(KD)]
        y = A.alloc([3, 4, HT], BF16)
        r_y = Res("y")
        yslot = S.slot()
        merged = A.alloc([KD, HT], BF16)
        r_m = [[Res("m") for t in range(2)] for k in range(KD)]
        sq = [A.alloc([TT], BF16) for _ in range(2)]
        sqres = [Res("sq") for _ in range(2)]
        rs = [A.alloc([TT], F32) for _ in range(2)]
        rsres = [Res("rs") for _ in range(2)]
        NW = 2
        gw = [A.alloc([KD, 3, 128], BF16) for _ in range(NW)]
        bw = [A.alloc([4, 3, 128], BF16) for _ in range(NW)]
        r_w = [Res("cw") for _ in range(NW)]
        wslot = [S.slot() for _ in range(NW)]
        ow = [A.alloc([KD, 128], BF16) for _ in range(NW)]
        r_ow = [Res("ow") for _ in range(NW)]
        owslot = [S.slot() for _ in range(NW)]
        sig = [A.alloc([TT], F32) for _ in range(3)]
        r_sig = [Res("sig") for _ in range(3)]
        tA = A.alloc([TT], F32)
        tB = A.alloc([TT], F32)
        r_tA, r_tB = Res("tA"), Res("tB")
        win = self.w["w_in"].ap()[l].rearrange("(k p) c -> p k c", p=128)
        wbr = self.w["w_branch"].ap()[l].rearrange("i (k p) d -> p k i d", p=128)
        wout = self.w["w_out"].ap()[l].rearrange("(k p) d -> p k d", p=128)
        wc = 0
        oc = 0
        for half in range(2):
            tiles = [2 * half, 2 * half + 1]
            for ti, t in enumerate(tiles):
                tsl = slice(t * TT, (t + 1) * TT)
                lsl = slice(ti * TT, (ti + 1) * TT)
                self.rmsnorm_tile([self.h[:, k, tsl] for k in range(KD)], [self.hres[k][t] for k in range(KD)],
                                  [u[:, k, lsl] for k in range(KD)], [ures[k][ti] for k in range(KD)],
                                  [self.gains[:, 16 + l * 8 + k:17 + l * 8 + k] for k in range(KD)], D,
                                  sq, sqres, rs, rsres, stat_bank=7)
            self.DMA("sp", yslot, [(y[:, br, :, :], self.dv(self.o_d, br * 512 * 2048 + half * HT, (2048, 128),
                                                           (128 * 2048, 4), (1, HT))) for br in range(3)],
                     writes=[r_y, self.r_od])
            for d in range(KD):
                wi = wc % NW
                wc += 1
                dsl = slice(d * 128, (d + 1) * 128)
                pairs = [(gw[wi][:, :, i, :], win[:, :, 2464 + i * 1024 + d * 128:2464 + i * 1024 + (d + 1) * 128])
                         for i in range(3)]
                pairs += [(bw[wi][:, :, i, :], wbr[:, :, i, dsl]) for i in range(3)]
                self.DMA("pool", wslot[wi], pairs, writes=[r_w[wi]])
                for ti, t in enumerate(tiles):
                    lsl = slice(ti * TT, (ti + 1) * TT)
                    gb = []
                    for i in range(3):
                        bk = self.bank()
                        gb.append(bk)
                        self.MM(ps[:, bk, :], [(gw[wi][:, k, i, :], u[:, k, lsl]) for k in range(KD)],
                                [ures[k][ti] for k in range(KD)] + [r_w[wi]], [self.pres[bk]])
                        self.ACT(sig[i], ps[:, bk, :], AF.Sigmoid, [self.pres[bk]], [r_sig[i]])
                    bb = []
                    for i in range(3):
                        bk = self.bank()
                        bb.append(bk)
                        self.MM(ps[:, bk, :], [(bw[wi][:, k, i, :], y[:, i, k, lsl]) for k in range(4)],
                                [r_y, r_w[wi]], [self.pres[bk]])
                    self.TT("dve", tA, sig[0], ps[:, bb[0], :], ALU.mult, [r_sig[0], self.pres[bb[0]]], [r_tA])
                    self.TT("dve", tB, sig[1], ps[:, bb[1], :], ALU.mult, [r_sig[1], self.pres[bb[1]]], [r_tB])
                    self.TT("dve", tA, tA, tB, ALU.add, [r_tA, r_tB], [r_tA])
                    self.TT("dve", tB, sig[2], ps[:, bb[2], :], ALU.mult, [r_sig[2], self.pres[bb[2]]], [r_tB])
                    self.TT("dve", merged[:, d, lsl], tA, tB, ALU.add, [r_tA, r_tB], [r_m[d][ti]])
            for d2 in range(KD):
                oi = oc % NW
                oc += 1
                self.DMA("pool", owslot[oi], [(ow[oi], wout[:, :, d2 * 128:(d2 + 1) * 128])], writes=[r_ow[oi]])
                for ti, t in enumerate(tiles):
                    tsl = slice(t * TT, (t + 1) * TT)
                    lsl = slice(ti * TT, (ti + 1) * TT)
                    bk = self.bank()
                    self.MM(ps[:, bk, :], [(ow[oi][:, k, :], merged[:, k, lsl]) for k in range(KD)],
                            [r_m[k][ti] for k in range(KD)] + [r_ow[oi]], [self.pres[bk]])
                    self.TT("dve", self.h[:, d2, tsl], self.h[:, d2, tsl], ps[:, bk, :], ALU.add,
                            [self.pres[bk], self.hres[d2][t]], [self.hres[d2][t]])
        A.pop()


def chunk_of(c, j):
    return (4 * j + c) if (j % 2 == 0) else (4 * j + 3 - c)


def token_index(c):
    idx = []
    for j in range(4):
        g = chunk_of(c, j)
        idx.append(np.arange(g * 512, (g + 1) * 512))
    return np.concatenate(idx)


def fm(v):
    v = np.asarray(v, np.float32)
    return np.ascontiguousarray(v.reshape(-1, 128).T)


_CACHE = {}
_LAST = {}


def _host_constants():
    tri = np.zeros((128, 4, 512), np.float32)
    k = np.arange(128)[:, None]
    q = np.arange(512)[None, :]
    for m in range(4):
        tri[:, m, :] = np.where(128 * m + k <= q, 0.0, -30000.0)
    band = np.zeros((4, 128, 4, 128), np.float32)
    tp = np.arange(128)[:, None]
    t = np.arange(128)[None, :]
    bands = np.zeros((3, 4, 128, 128), np.float32)
    for g, w in enumerate((2, 4, 8, 16)):
        inwin = ((t - tp) >= 0) & ((t - tp) <= w - 1)
        bands[0, g] = inwin / w - (t == tp)
        bands[1, g] = (tp >= t + 129 - w) / w
        cntv = np.minimum(t + 1, w).astype(np.float32)
        bands[2, g] = inwin / cntv - (t == tp)
    return tri, bands


def kernel(**inputs):
    stage = inputs.pop("_stage", 99)
    x = np.asarray(inputs["x"], np.float32)
    pos = np.asarray(inputs["positions"]).astype(np.int32)
    if stage not in _CACHE:
        _CACHE[stage] = Prog(stage)
    prog = _CACHE[stage]
    f32 = lambda a: np.ascontiguousarray(np.asarray(a, np.float32))

    gains = np.zeros((128, 64), np.float32)
    ptab = np.zeros((128, 32), np.float32)
    lam = np.zeros((128, 8, 32), np.float32)
    for l in range(DEPTH):
        gains[:, 0 + l * 8:8 + l * 8] = fm(inputs["ffn1_norm"][l])
        gains[:, 16 + l * 8:24 + l * 8] = fm(inputs["mix_norm"][l])
        gains[:, 32 + l * 8:40 + l * 8] = fm(inputs["ffn2_norm"][l])
        gains[:, 56 + 2 * l:58 + 2 * l] = fm(inputs["mla_q_norm"][l])
        gains[:, 60 + l] = np.asarray(inputs["mla_kv_norm"][l], np.float32)
        gains[:, 62 + l] = np.tile(np.asarray(inputs["diff_subln"][l], np.float32), 2)
        ptab[:, l * 4:l * 4 + 4] = np.asarray(inputs["pool_b"][l], np.float32).T
        ptab[:, 8 + l * 4:12 + l * 4] = fm(inputs["pool_scale"][l])
        for i, nm in enumerate(("diff_lambda_q1", "diff_lambda_k1", "diff_lambda_q2", "diff_lambda_k2")):
            lam[:, l * 4 + i, :] = np.asarray(inputs[nm][l], np.float32)[None, :]
    gains[:, 48:56] = fm(inputs["final_norm"])
    inv_freq = (np.float32(10000.0) ** (-np.arange(16, dtype=np.float32) / np.float32(16))).astype(np.float32)
    ptab[64:96, 16] = np.tile(inv_freq, 2)
    ptab[64:80, 17] = -1.0
    ptab[80:96, 17] = 1.0
    tri, bands = _host_constants()
    common = {
        "gains": gains, "ptab": ptab, "lam": lam,
        "ones": np.ones((128, 128), np.float32),
        "tri": tri, "ident": np.eye(128, dtype=np.float32),
    }
    for nm in ("ffn1_w_gate", "ffn1_w_up", "ffn1_w_down", "ffn2_w_gate", "ffn2_w_up", "ffn2_w_down",
               "w_in", "mla_w_uq", "mla_w_ukv", "pool_w", "w_branch", "w_out"):
        if nm in prog.din:
            common[nm] = f32(inputs[nm])
    in_maps = []
    for core in range(NCORES):
        b, c = divmod(core, 4)
        idx = token_index(c)
        m = dict(common)
        m["xT"] = np.ascontiguousarray(x[b, idx, :].T)
        if "pos_own" in prog.din:
            po = pos[b, idx]
            m["pos_own"] = np.ascontiguousarray(np.broadcast_to(po[None, :], (32, T)))
            m["pos_own16"] = np.ascontiguousarray(po.reshape(128, 16))
            gidx = np.concatenate([token_index(r) for r in range(4)])
            m["pos_g64"] = np.ascontiguousarray(pos[b, gidx].reshape(128, 64))
            fl = np.zeros((4, 8192), np.float32)
            for r in range(4):
                for j2 in range(4):
                    if chunk_of(r, j2) >= chunk_of(c, j2):
                        fl[j2, r * 2048 + j2 * 512:r * 2048 + (j2 + 1) * 512] = 1.0
            m["flags"] = np.ascontiguousarray(fl.reshape(4, 128, 64))
            bd = np.zeros((128, 3, 4, 128), np.float32)
            for g in range(4):
                bd[:, 0, g, :] = bands[0, g]
                bd[:, 1, g, :] = bands[1, g]
                bd[:, 2, g, :] = bands[2, g] if c == 0 else bands[0, g]
            m["band"] = bd
            ts = np.zeros((128, 2, 4, 4, 128), np.float32)
            for j in range(4):
                G = chunk_of(c, j)
                if G == 0:
                    continue
                found = [(r, j2) for r in range(4) for j2 in range(4) if chunk_of(r, j2) == G - 1]
                r, j2 = found[0]
                for g, w in enumerate((2, 4, 8, 16)):
                    for tok in range(16):
                        for tt in range(w - 1):
                            if tok >= tt - w + 17:
                                ts[(r % 2) * 64 + j2 * 16 + tok, r // 2, j, g, tt] = 1.0 / w
            m["tailsel"] = ts
        in_maps.append({k: v for k, v in m.items() if k in prog.din})
    res = run_bass_kernel_spmd(prog.nc, in_maps, core_ids=list(range(NCORES)))
    _LAST["res"] = res.results
    out = np.empty((B, S, D), np.float32)
    for core in range(NCORES):
        b, c = divmod(core, 4)
        out[b, token_index(c), :] = np.asarray(res.results[core]["outT"]).T
    return out
```
